# Optimizing a Trainium2 kernel written in Bass

```python
import math
import jax, jax.numpy as jnp
from jax import lax
import numpy as np

D_MODEL = 2048
BATCH = 16
SEQ = 2048
DEPTH = 1
DEC_BATCH = 4
DEC_SEQ = 4096
PAST_LEN = 128

MIX_WIDTH = D_MODEL
CONV_WIDTH = D_MODEL // 2
RET_WIDTH = MIX_WIDTH - CONV_WIDTH
RET_HEADS = 8
RET_HEAD_DIM = RET_WIDTH // RET_HEADS
CONV_KERNEL = 31
CONV_PAD = CONV_KERNEL // 2
CHUNK = 128
D_FF = -(-8 * D_MODEL // (3 * 256)) * 256
ROPE_BASE = 10000.0
EPS = 1e-6
IN_COLS = 2 * CONV_WIDTH + 4 * RET_WIDTH

kernel_name = "hymba_conformer_retnet_encoder"


def rms_norm(x, w):
    xf = x.astype(jnp.float32)
    y = xf * lax.rsqrt(jnp.mean(xf * xf, axis=-1, keepdims=True) + EPS)
    return (y * w.astype(jnp.float32)).astype(x.dtype)


def layer_norm(x, w, b):
    xf = x.astype(jnp.float32)
    mu = jnp.mean(xf, axis=-1, keepdims=True)
    var = jnp.mean(jnp.square(xf - mu), axis=-1, keepdims=True)
    y = (xf - mu) * lax.rsqrt(var + EPS)
    return (y * w.astype(jnp.float32) + b.astype(jnp.float32)).astype(x.dtype)


def rotary(x):
    S, d = x.shape[2], x.shape[3]
    inv_freq = ROPE_BASE ** (-jnp.arange(0, d, 2, dtype=jnp.float32) / d)
    ang = jnp.arange(S, dtype=jnp.float32)[:, None] * inv_freq[None, :]
    cos, sin = jnp.cos(ang), jnp.sin(ang)
    x1, x2 = x[..., : d // 2], x[..., d // 2:]
    return jnp.concatenate([x1 * cos - x2 * sin, x1 * sin + x2 * cos], axis=-1)


def conv_mixer(u, conv_w, conv_b, ln_w, ln_b):
    a, g = u[..., :CONV_WIDTH], u[..., CONV_WIDTH:]
    h = a * jax.nn.sigmoid(g)
    h = lax.conv_general_dilated(
        h, conv_w.reshape(CONV_KERNEL, 1, CONV_WIDTH).astype(h.dtype),
        window_strides=(1,), padding=[(CONV_PAD, CONV_PAD)],
        dimension_numbers=("NWC", "WIO", "NWC"),
        feature_group_count=CONV_WIDTH) + conv_b.astype(h.dtype)
    h = layer_norm(h, ln_w, ln_b)
    return jax.nn.silu(h)


def retention_direction(q, k, v, log_g, strict):
    B, H, S, d = q.shape
    N = S // CHUNK
    qc = q.reshape(B, H, N, CHUNK, d)
    kc = k.reshape(B, H, N, CHUNK, d)
    vc = v.reshape(B, H, N, CHUNK, d)
    idx = jnp.arange(CHUNK, dtype=jnp.float32)
    diff = idx[:, None] - idx[None, :]
    mask = diff > 0 if strict else diff >= 0
    lg = log_g.astype(jnp.float32)
    D = jnp.where(mask[None], jnp.exp(jnp.where(mask, diff, 0.0)[None] * lg[:, None, None]), 0.0)
    scores = jnp.einsum('bhncd,bhnmd->bhncm', qc, kc) * D[None, :, None]
    inner = jnp.einsum('bhncm,bhnme->bhnce', scores, vc)
    zeta = jnp.exp((CHUNK - 1 - idx)[None, :] * lg[:, None])
    xi = jnp.exp((idx + 1)[None, :] * lg[:, None])
    kv = jnp.einsum('bhncd,bhnce->nbhde', kc * zeta[None, :, None, :, None], vc)
    chunk_decay = jnp.exp(CHUNK * lg)[None, :, None, None]

    def step(R, kv_n):
        return chunk_decay * R + kv_n, R

    R0 = jnp.zeros((B, H, d, d), jnp.float32)
    _, R_prev = lax.scan(step, R0, kv)
    cross = jnp.einsum('bhncd,nbhde->bhnce', qc, R_prev) * xi[None, :, None, :, None]
    return (inner + cross).reshape(B, H, S, d)


def retention_mixer(q, k, v, g, log_g_fwd, log_g_bwd, norm_w):
    B, S, _ = q.shape
    dt = q.dtype

    def heads(t):
        return t.astype(jnp.float32).reshape(B, S, RET_HEADS, RET_HEAD_DIM).transpose(0, 2, 1, 3)

    qh = rotary(heads(q))
    kh = rotary(heads(k)) * (RET_HEAD_DIM ** -0.5)
    vh = heads(v)
    fwd = retention_direction(qh, kh, vh, log_g_fwd, False)
    bwd = jnp.flip(retention_direction(jnp.flip(qh, 2), jnp.flip(kh, 2), jnp.flip(vh, 2), log_g_bwd, True), 2)
    o = fwd + bwd
    o = o * lax.rsqrt(jnp.mean(o * o, axis=-1, keepdims=True) + EPS)
    o = o * norm_w.astype(jnp.float32).reshape(RET_HEADS, 1, RET_HEAD_DIM)[None]
    o = o.transpose(0, 2, 1, 3).reshape(B, S, RET_WIDTH).astype(dt)
    return jax.nn.silu(g) * o


def block(x, norm1_w, w_in, conv_w, conv_b, conv_ln_w, conv_ln_b,
          ret_log_decay_fwd, ret_log_decay_bwd, ret_norm_w, w_out,
          norm2_w, w_ffn_in, w_ffn_out):
    h = rms_norm(x, norm1_w)
    u = h @ w_in.astype(h.dtype)
    c0 = 2 * CONV_WIDTH
    conv_u = u[..., :c0]
    q = u[..., c0:c0 + RET_WIDTH]
    k = u[..., c0 + RET_WIDTH:c0 + 2 * RET_WIDTH]
    v = u[..., c0 + 2 * RET_WIDTH:c0 + 3 * RET_WIDTH]
    g = u[..., c0 + 3 * RET_WIDTH:]
    y_conv = conv_mixer(conv_u, conv_w, conv_b, conv_ln_w, conv_ln_b)
    y_ret = retention_mixer(q, k, v, g, ret_log_decay_fwd, ret_log_decay_bwd, ret_norm_w)
    x = x + jnp.concatenate([y_conv, y_ret], axis=-1) @ w_out.astype(x.dtype)
    h2 = rms_norm(x, norm2_w)
    gu = h2 @ w_ffn_in.astype(h2.dtype)
    x = x + (jax.nn.silu(gu[..., :D_FF]) * gu[..., D_FF:]) @ w_ffn_out.astype(x.dtype)
    return x


def trunk(x, norm1_w, w_in, conv_w, conv_b, conv_ln_w, conv_ln_b,
          ret_log_decay_fwd, ret_log_decay_bwd, ret_norm_w, w_out,
          norm2_w, w_ffn_in, w_ffn_out, final_norm_w):
    for l in range(DEPTH):
        x = block(x, norm1_w[l], w_in[l], conv_w[l], conv_b[l], conv_ln_w[l], conv_ln_b[l],
                  ret_log_decay_fwd[l], ret_log_decay_bwd[l], ret_norm_w[l], w_out[l],
                  norm2_w[l], w_ffn_in[l], w_ffn_out[l])
    return rms_norm(x, final_norm_w)


def setup_inputs(seed: int = 0) -> dict:
    key = jax.random.key(seed)
    ks = jax.random.split(key, 20)
    f32 = jnp.float32
    base = jnp.log(1.0 - 2.0 ** (-5.0 - jnp.arange(RET_HEADS, dtype=f32)))
    return {
        "x_prompt": jax.random.normal(ks[0], (BATCH, SEQ, D_MODEL), f32),
        "x_sample": jax.random.normal(ks[1], (DEC_BATCH, DEC_SEQ, D_MODEL), f32),
        "norm1_w": 1.0 + 0.01 * jax.random.normal(ks[2], (DEPTH, D_MODEL), f32),
        "w_in": jax.random.normal(ks[3], (DEPTH, D_MODEL, IN_COLS), f32) * D_MODEL ** -0.5,
        "conv_w": jax.random.normal(ks[4], (DEPTH, CONV_KERNEL, CONV_WIDTH), f32) * CONV_KERNEL ** -0.5,
        "conv_b": 0.01 * jax.random.normal(ks[5], (DEPTH, CONV_WIDTH), f32),
        "conv_ln_w": 1.0 + 0.01 * jax.random.normal(ks[6], (DEPTH, CONV_WIDTH), f32),
        "conv_ln_b": 0.01 * jax.random.normal(ks[7], (DEPTH, CONV_WIDTH), f32),
        "ret_log_decay_fwd": base[None] * (1.0 + 0.1 * jax.random.uniform(ks[8], (DEPTH, RET_HEADS), f32)),
        "ret_log_decay_bwd": base[None] * (1.0 + 0.1 * jax.random.uniform(ks[9], (DEPTH, RET_HEADS), f32)),
        "ret_norm_w": 1.0 + 0.01 * jax.random.normal(ks[10], (DEPTH, RET_WIDTH), f32),
        "w_out": jax.random.normal(ks[11], (DEPTH, MIX_WIDTH, D_MODEL), f32) * MIX_WIDTH ** -0.5,
        "norm2_w": 1.0 + 0.01 * jax.random.normal(ks[12], (DEPTH, D_MODEL), f32),
        "w_ffn_in": jax.random.normal(ks[13], (DEPTH, D_MODEL, 2 * D_FF), f32) * D_MODEL ** -0.5,
        "w_ffn_out": jax.random.normal(ks[14], (DEPTH, D_FF, D_MODEL), f32) * D_FF ** -0.5,
        "final_norm_w": 1.0 + 0.01 * jax.random.normal(ks[15], (D_MODEL,), f32),
    }


def reference(x_prompt, x_sample, norm1_w, w_in, conv_w, conv_b, conv_ln_w, conv_ln_b,
              ret_log_decay_fwd, ret_log_decay_bwd, ret_norm_w, w_out,
              norm2_w, w_ffn_in, w_ffn_out, final_norm_w):
    y_prompt = trunk(x_prompt, norm1_w, w_in, conv_w, conv_b, conv_ln_w, conv_ln_b,
                     ret_log_decay_fwd, ret_log_decay_bwd, ret_norm_w, w_out,
                     norm2_w, w_ffn_in, w_ffn_out, final_norm_w)
    y_sample = trunk(x_sample, norm1_w, w_in, conv_w, conv_b, conv_ln_w, conv_ln_b,
                     ret_log_decay_fwd, ret_log_decay_bwd, ret_norm_w, w_out,
                     norm2_w, w_ffn_in, w_ffn_out, final_norm_w)
    return (y_prompt, y_sample)
```

```python
import math
from contextlib import ExitStack

import numpy as np
import concourse.bass as bass
import concourse.mybir as mybir
from concourse.bass_utils import run_bass_kernel_spmd

F32 = mybir.dt.float32
BF16 = mybir.dt.bfloat16
I32 = mybir.dt.int32
AF = mybir.ActivationFunctionType
ALU = mybir.AluOpType

EPS = 1e-6
PG = 512
NCORES = 8


class View:
    __slots__ = ("ap", "space", "lo", "hi")

    def __init__(self, ap, space, lo, hi):
        self.ap, self.space, self.lo, self.hi = ap, space, lo, hi

    def m(self, fn):
        return View(fn(self.ap), self.space, self.lo, self.hi)


class Buf:
    def __init__(self, ap, space, off, shape, esz):
        self.ap, self.space, self.off, self.shape, self.esz = ap, space, off, tuple(shape), esz
        st = [1]
        for d in reversed(self.shape[2:]):
            st.insert(0, st[0] * d)
        self.strides = st

    def __getitem__(self, key):
        if not isinstance(key, tuple):
            key = (key,)
        key = key + (slice(None),) * (len(self.shape) - len(key))
        lo = hi = 0
        for k, n, s in zip(key[1:], self.shape[1:], self.strides):
            if isinstance(k, slice):
                a = 0 if k.start is None else k.start
                b = n if k.stop is None else k.stop
            else:
                a, b = k, k + 1
            assert 0 <= a < b <= n, (key, self.shape)
            lo += a * s
            hi += (b - 1) * s
        return View(self.ap[key], self.space, self.off + lo * self.esz, self.off + (hi + 1) * self.esz)


def dview(ap, name):
    return View(ap, "dr:" + name, 0, 1)


class K:
    ENG = ("pe", "act", "dve", "pool", "sp")

    def __init__(self):
        self.ops = []
        self.lw = {}
        self.rd = {}
        self.nops = 0

    @staticmethod
    def pages(v):
        if v.space.startswith("dr"):
            return [(v.space, 0)]
        return [(v.space, p) for p in range(v.lo // PG, (v.hi - 1) // PG + 1)]

    def op(self, eng, fn, reads=(), writes=(), dma_sem=None, dur=0.3):
        deps = set()
        for v in reads:
            for p in self.pages(v):
                w = self.lw.get(p)
                if w is not None:
                    deps.add(w)
        for v in writes:
            for p in self.pages(v):
                w = self.lw.get(p)
                if w is not None:
                    deps.add(w)
                r = self.rd.get(p)
                if r:
                    deps.update(r)
        i = len(self.ops)
        self.ops.append((eng, fn, tuple(deps), dma_sem, dur))
        self.nops += 1
        for v in reads:
            for p in self.pages(v):
                self.rd.setdefault(p, set()).add(i)
        for v in writes:
            for p in self.pages(v):
                self.lw[p] = i
                self.rd[p] = set()

    def schedule(self, reorder=True):
        import heapq
        ops = self.ops
        n = len(ops)
        order = {e: [] for e in self.ENG}
        if not reorder:
            for i, o in enumerate(ops):
                order[o[0]].append(i)
            return order
        indeg = [len(o[2]) for o in ops]
        users = [[] for _ in range(n)]
        for i, o in enumerate(ops):
            for d in o[2]:
                users[d].append(i)
        ready = {e: [] for e in self.ENG}
        busy = {e: False for e in self.ENG}
        ev = []
        seq = [0]

        def push(t, kind, i):
            seq[0] += 1
            heapq.heappush(ev, (t, seq[0], kind, i))

        LAT = 0.15
        for i in range(n):
            if indeg[i] == 0:
                heapq.heappush(ready[ops[i][0]], i)

        def try_start(e, now):
            if busy[e] or not ready[e]:
                return
            i = heapq.heappop(ready[e])
            busy[e] = True
            order[e].append(i)
            o = ops[i]
            if o[3] is not None:
                push(now + 0.06, 1, i)
                push(now + o[4], 0, i)
            else:
                push(now + o[4], 0, i)

        for e in self.ENG:
            try_start(e, 0.0)
        while ev:
            t, _, kind, i = heapq.heappop(ev)
            e = ops[i][0]
            if kind == 2:
                heapq.heappush(ready[e], i)
                try_start(e, t)
                continue
            if kind == 1:
                busy[e] = False
                try_start(e, t)
                continue
            if ops[i][3] is None:
                busy[e] = False
            for u in users[i]:
                indeg[u] -= 1
                if indeg[u] == 0:
                    push(t + LAT, 2, u)
            try_start(e, t)
        assert sum(len(v) for v in order.values()) == n, "scheduler lost ops (cycle?)"
        self.sim_time = t
        return order

    def emit(self, nc, es, reorder=True):
        ops = self.ops
        order = self.schedule(reorder)
        semval = [None] * len(ops)
        cnt = {}
        for e in self.ENG:
            for i in order[e]:
                o = ops[i]
                if o[3] is not None:
                    sn, inc = o[3], 16
                else:
                    sn, inc = "e_" + e, 1
                cnt[sn] = cnt.get(sn, 0) + inc
                semval[i] = (sn, cnt[sn], inc)
        sems = {sn: es.enter_context(nc.semaphore(sn)) for sn in cnt}
        streams = {}
        for e in self.ENG:
            own = "e_" + e
            waited = {}
            items = []
            for i in order[e]:
                need = {}
                for d in ops[i][2]:
                    sn, val, _ = semval[d]
                    if sn == own and e == "pe":
                        continue
                    if need.get(sn, 0) < val:
                        need[sn] = val
                for sn, val in need.items():
                    if waited.get(sn, 0) >= val:
                        continue
                    waited[sn] = val
                    items.append(("w", sn, val))
                items.append(("o", ops[i][1], semval[i][0], semval[i][2]))
            streams[e] = items
        for sn, val in cnt.items():
            if not sn.startswith("e_"):
                streams["sp"].append(("w", sn, val))
        block = es.enter_context(nc.Block())

        def runner(items):
            def run(eng):
                for it in items:
                    if it[0] == "w":
                        eng.wait_ge(sems[it[1]], it[2])
                    else:
                        it[1](eng).then_inc(sems[it[2]], it[3])
            return run

        block.tensor(runner(streams["pe"]))
        block.scalar(runner(streams["act"]))
        block.vector(runner(streams["dve"]))
        block.gpsimd(runner(streams["pool"]))
        block.sync(runner(streams["sp"]))


def make_cfg(D=2048, DFF=5632, SEGB=4, NSEG=3):
    c = dict(D=D, DFF=DFF, SEGB=SEGB, NSEG=NSEG)
    c["KT"] = D // 128
    c["CW"] = D // 2
    c["RW"] = D // 2
    c["CT"] = c["CW"] // 128
    c["NH"] = c["RW"] // 128
    c["FT"] = DFF // 128
    c["NB"] = SEGB * NSEG
    c["NCH"] = c["NB"] * 4
    c["NTOK"] = c["NB"] * 512
    c["INC"] = 2 * c["CW"] + 4 * c["RW"]
    return c


def build_nc(cfg):
    D, DFF, KT, CW, RW, CT, NH, FT = (cfg[k] for k in ("D", "DFF", "KT", "CW", "RW", "CT", "NH", "FT"))
    NB, NCH, NTOK, SEGB, INC = (cfg[k] for k in ("NB", "NCH", "NTOK", "SEGB", "INC"))
    KH = KT // 2
    assert RW % 512 == 0 and D % 512 == 0 and CT % 2 == 0 and NH % 4 == 0 and KT <= 16
    NHG = NH // 4
    F2K = [min(8, FT - q * 8) for q in range((FT + 7) // 8)]
    KQ = len(F2K)
    CPS = SEGB * 4
    nc = bass.Bass("TRN2", target_bir_lowering=False)
    kk = K()
    es = ExitStack()

    def din(name, shape, dt=F32):
        return nc.dram_tensor(name, list(shape), dt, kind="ExternalInput").ap()

    x_d = din("x", [NTOK, D])
    w_in_d = din("w_in", [D, INC])
    w_out_d = din("w_out", [D, D])
    w_f1_d = din("w_ffn_in", [D, 2 * DFF])
    w_f2_d = din("w_ffn_out", [DFF, D])
    conv_w_d = din("conv_w", [31, CW])
    conv_b_d = din("conv_b", [1, CW])
    ln_w_d = din("conv_ln_w", [1, CW])
    ln_b_d = din("conv_ln_b", [1, CW])
    lgf_d = din("lgf", [NH])
    lgb_d = din("lgb", [NH])
    rnw_d = din("ret_norm_w", [NH, 128])
    n1_d = din("norm1_w", [KT, 128])
    n2_d = din("norm2_w", [KT, 128])
    nf_d = din("final_norm_w", [D])
    ident_d = din("ident", [128, 128])
    link_d = din("link", [128, 1])
    cos_d = din("cos_t", [NCH, 128, 64])
    sin_d = din("sin_t", [NCH, 128, 64])
    y_d = nc.dram_tensor("y", [NTOK, D], F32, kind="ExternalOutput").ap()

    def dscr(name, n, elems):
        return nc.dram_tensor(name, [n, 128, elems], BF16, kind="Internal").ap()

    n_cv = 2 * CW // 256
    n_qkv = (3 * RW // 512) * 2
    n_g = RW // 256
    n_wo = (D // 512) * 2
    n_f1 = 2 * DFF // 256
    n_f2 = (D // 512) * KQ
    s_cv = dscr("s_cv", n_cv, KT * 256)
    s_qkv = dscr("s_qkv", n_qkv, KH * 512)
    s_g = dscr("s_g", n_g, KT * 256)
    s_wo = dscr("s_wo", n_wo, KH * 512)
    s_f1 = dscr("s_f1", n_f1, KT * 256)
    s_f2 = dscr("s_f2", n_f2, 8 * 512)
    rbs_d = nc.dram_tensor("s_rb", [NCH, 128, NH * 128], BF16, kind="Internal").ap()
    kvs_d = nc.dram_tensor("s_kv", [NB, 2, 128, 4 * RW], BF16, kind="Internal").ap()

    pos = [0]
    base_t = es.enter_context(nc.sbuf_tensor("arena", [128, 206 * 1024], mybir.dt.uint8))
    arena_off = nc.lookup_mloc(base_t).addr
    cnt_alloc = [0]

    arena_ap = base_t.ap()

    def alloc_at(off, shape, dt):
        esz = 2 if dt == BF16 else 4
        n = 1
        for d in shape[1:]:
            n *= d
        ap = arena_ap[:, off:off + n * esz].bitcast(dt)
        if len(shape) == 3:
            ap = ap.rearrange("p (a b) -> p a b", a=shape[1])
        elif len(shape) == 4:
            ap = ap.rearrange("p (a b c) -> p a b c", a=shape[1], b=shape[2])
        return Buf(ap, "sb", off, shape, esz)

    def alloc(shape, dt, align=PG):
        esz = 2 if dt == BF16 else 4
        n = 1
        for d in shape[1:]:
            n *= d
        off = (pos[0] + align - 1) // align * align
        pos[0] = off + n * esz
        assert pos[0] <= 206 * 1024, "SBUF arena overflow %d" % pos[0]
        return alloc_at(off, shape, dt)

    DcT = alloc([128, NH, 128], F32)
    XIf = alloc([128, NH, 128], F32)
    XIb = alloc([128, NH, 128], F32)
    wfb = alloc([128, D], F32)
    identf = alloc([128, 128], F32)
    identb = alloc([128, 128], BF16)
    ones_ln = alloc([128, 128], F32)
    ones_gn = alloc([128, 128], F32)
    cw = alloc([128, CT, 36], F32)
    small = alloc([128, 128], F32)
    o = [0]

    def sm(n):
        v = (o[0], o[0] + n)
        o[0] += n
        assert o[0] <= 128
        return v

    c_rnw, c_n1, c_n2, c_lgf, c_lgb, c_zf, c_zb, c_cdf, c_cdb = (sm(NH), sm(KT), sm(KT), sm(NH), sm(NH), sm(NH), sm(NH), sm(NH), sm(NH))
    c_link, c_pj, c_pj2, c_ss, c_rs = sm(1), sm(1), sm(1), sm(5), sm(5)

    def S(c, i=None):
        if i is None:
            return small[:, c[0]:c[1]]
        return small[:, c[0] + i:c[0] + i + 1]

    Rf32 = alloc([128, NH, 128], F32)
    Rf_bf = alloc([128, NH, 128], BF16)
    Rb_bf = [alloc([128, NH, 128], BF16) for _ in range(2)]
    SLOT = 8 * 512 * 2
    assert KT * 256 * 2 <= SLOT and KH * 512 * 2 <= SLOT
    NSLOT = 4
    ring = [alloc([128, SLOT // 2], BF16) for _ in range(NSLOT)]
    cos4 = alloc([128, 4, 64], F32)
    sin4 = alloc([128, 4, 64], F32)
    tmp = [alloc([128, 512], F32) for _ in range(4)]
    rt = [alloc_at(tmp[3].off + i * 1024, [128, 4, 64], F32) for i in range(2)]
    h_bf = alloc([128, D], BF16)
    xs = alloc([128, D], F32)
    hT = alloc([128, KT, 512], BF16)
    hTh = alloc([128, KT, 32], BF16)
    qT = alloc([128, NH, 128], BF16)
    qfT = alloc([128, NH, 128], BF16)
    qbT = alloc([128, NH, 128], BF16)
    kT = alloc([128, NH, 128], BF16)
    PTs = [alloc([128, 4, 128], BF16) for _ in range(2)]
    kzs = [alloc([128, 4, 128], BF16) for _ in range(2)]
    xr_off = (pos[0] + PG - 1) // PG * PG
    xr_size = max(4 * D * 4, CT * 542 * 4 + CT * 512 * 4 + PG)
    pos[0] = xr_off + xr_size
    xt = alloc_at(xr_off, [128, 4, D], F32)
    glu = alloc_at(xr_off, [128, CT, 542], F32)
    acc_off = (xr_off + CT * 542 * 4 + PG - 1) // PG * PG
    acc = alloc_at(acc_off, [128, CT, 512], F32)
    q_rot = alloc([128, 4, NH, 128], BF16)
    m_off = (pos[0] + PG - 1) // PG * PG
    ymix_sz = KT * 512 * 2
    sgT_sz = NH * 512 * 2
    m_size = max(ymix_sz + max(sgT_sz, D * 4) + 2 * 4 * RW * 2 + 4096, FT * 512 * 2)
    pos[0] = m_off + m_size
    assert pos[0] <= 206 * 1024, "SBUF arena overflow %d" % pos[0]
    ymix = alloc_at(m_off, [128, KT, 512], BF16)
    sgT = alloc_at(m_off + ymix_sz, [128, NH, 512], BF16)
    xh = alloc_at(m_off + ymix_sz, [128, D], F32)
    actT = alloc_at(m_off, [128, FT, 512], BF16)
    kv_off = m_off + ymix_sz + max(sgT_sz, D * 4)
    k_rot = alloc_at(kv_off, [128, 4, NH, 128], BF16)
    v_sb = alloc_at(kv_off + 4 * RW * 2, [128, 4, NH, 128], BF16)
    h_bf2 = alloc_at(kv_off + 8 * RW * 2, [128, D], BF16)
    Rb32 = alloc_at(m_off + ymix_sz, [128, NH, 128], F32)
    hT2 = alloc_at(m_off, [128, KT, 512], BF16)
    stg = alloc_at(xr_off, [128, max(CW, 128)], F32)
    stg2 = alloc_at(xr_off + max(CW, 128) * 4, [128, 128], F32)
    dif = alloc_at(acc_off, [128, 128], F32)
    difi = alloc_at(acc_off + 512, [128, 128], I32)

    ps_t = [es.enter_context(nc.psum_tensor("ps%d" % i, [128, 512], F32)) for i in range(8)]

    def psap(t):
        return t.ap() if hasattr(t, "ap") and callable(getattr(t, "ap")) else t

    PB = [Buf(psap(ps_t[i]), "ps", i * 2048, [128, 512], 4) for i in range(8)]
    PB4 = [Buf(psap(ps_t[i]).rearrange("p (a b) -> p a b", a=4), "ps", i * 2048, [128, 4, 128], 4) for i in range(8)]
    PBb = [Buf(psap(ps_t[i]).bitcast(BF16).rearrange("p (a b) -> p a b", b=128), "ps", i * 2048, [128, 8, 128], 2) for i in range(8)]
    TR = [0, 1]
    MM = [2, 3, 4, 5]
    RT0, RT1 = 6, 7
    rot = {"tr": 0, "mm": 0, "slot": 0, "tmp": 0}

    def nxt(key, lst):
        i = rot[key]
        rot[key] = (i + 1) % len(lst)
        return lst[i]

    def fsz(v):
        n = 1
        for d in v.ap.shape[1:]:
            n *= d
        return n

    def bank(v):
        lo = v.lo // 2048 * 2048
        return View(v.ap, "ps", lo, lo + 2048)

    def mm(out, lhsT, rhs, start=True, stop=True):
        d = max(fsz(rhs), 64) / 2300.0 * (4.0 if rhs.ap.dtype == F32 else 1.0) + 0.005
        kk.op("pe", lambda e: e.matmul(out.ap, lhsT=lhsT.ap, rhs=rhs.ap, start=start, stop=stop), reads=[lhsT, rhs], writes=[bank(out)], dur=d)

    def trp(out, in_, ident):
        kk.op("pe", lambda e: e.transpose(out.ap, in_.ap, ident.ap), reads=[in_, ident], writes=[bank(out)], dur=0.1 * (4.0 if in_.ap.dtype == F32 else 1.0))

    def edur(eng, out):
        n = fsz(out)
        if eng == "act":
            return 0.2 + n / 1200.0
        if eng == "pool":
            return 0.5 + n / 200.0
        return 0.1 + n / 900.0

    def act(out, in_, func, scale=1.0, bias=0.0, accum=None, eng="act"):
        rds = [in_]
        kw = {}
        if isinstance(scale, View):
            rds.append(scale)
            kw["scale"] = scale.ap
        else:
            kw["scale"] = float(scale)
        if isinstance(bias, View):
            rds.append(bias)
            kw["bias"] = bias.ap
        elif bias != 0.0:
            kw["bias"] = float(bias)
        wr = [out]
        if accum is not None:
            wr.append(accum)
            kw["accum_out"] = accum.ap
        kk.op(eng, lambda e: e.activation(out=out.ap, in_=in_.ap, func=func, **kw), reads=rds, writes=wr, dur=edur(eng, out))

    def tt(out, in0, in1, op, eng="dve"):
        kk.op(eng, lambda e: e.tensor_tensor(out=out.ap, in0=in0.ap, in1=in1.ap, op=op), reads=[in0, in1], writes=[out], dur=edur(eng, out))

    def ts(out, in0, s1, s2, op0, op1=None, eng="dve"):
        rds = [in0]
        a1 = s1.ap if isinstance(s1, View) else float(s1)
        a2 = None if s2 is None else (s2.ap if isinstance(s2, View) else float(s2))
        for s in (s1, s2):
            if isinstance(s, View):
                rds.append(s)
        if op1 is None:
            kk.op(eng, lambda e: e.tensor_scalar(out=out.ap, in0=in0.ap, scalar1=a1, scalar2=None, op0=op0), reads=rds, writes=[out], dur=edur(eng, out))
        else:
            kk.op(eng, lambda e: e.tensor_scalar(out=out.ap, in0=in0.ap, scalar1=a1, scalar2=a2, op0=op0, op1=op1), reads=rds, writes=[out], dur=edur(eng, out))

    def stt(out, in0, sc, in1, op0, op1):
        rds = [in0, in1]
        a = sc.ap if isinstance(sc, View) else float(sc)
        if isinstance(sc, View):
            rds.append(sc)
        kk.op("dve", lambda e: e.scalar_tensor_tensor(out=out.ap, in0=in0.ap, scalar=a, in1=in1.ap, op0=op0, op1=op1), reads=rds, writes=[out], dur=edur("dve", out))

    def cp(out, in_, eng="dve"):
        if eng == "act":
            act(out, in_, AF.Copy)
        else:
            kk.op(eng, lambda e: e.tensor_copy(out=out.ap, in_=in_.ap), reads=[in_], writes=[out], dur=edur(eng, out))

    def mset(v, val, eng="dve"):
        kk.op(eng, lambda e: e.memset(v.ap, val), writes=[v], dur=edur(eng, v))

    def recip(out, in_):
        kk.op("dve", lambda e: e.reciprocal(out=out.ap, in_=in_.ap), reads=[in_], writes=[out], dur=edur("dve", out))

    def recip_big(out, in_, scratch):
        kk.op("act", lambda e: e.activation(out=scratch.ap, in_=in_.ap, func=AF.Ln), reads=[in_], writes=[scratch], dur=edur("act", out))
        kk.op("act", lambda e: e.activation(out=out.ap, in_=scratch.ap, func=AF.Exp, scale=-1.0), reads=[scratch], writes=[out], dur=edur("act", out))

    def dma(eng, out, in_, sem):
        v = out if out.ap is not None and not out.space.startswith("dr") else in_
        nb = fsz(v) * (2 if v.ap.dtype == BF16 else 4) * v.ap.shape[0]
        kk.op(eng, lambda e: e.dma_start(out=out.ap, in_=in_.ap), reads=[in_], writes=[out], dma_sem=sem, dur=2.0 + nb / 180e3)

    def bc(v, shape, axis):
        return v.m(lambda a: a.unsqueeze(axis).to_broadcast(list(shape)))

    dma("act", identf[:], dview(ident_d, "ident"), "d_c0")
    dma("act", small[:, c_link[0]:c_link[1]], dview(link_d, "link"), "d_c1")
    dma("act", wfb[:], dview(nf_d.partition_broadcast(128), "nf"), "d_c2")
    dma("act", S(c_lgf), dview(lgf_d.partition_broadcast(128), "lgf"), "d_c3")
    dma("act", S(c_lgb), dview(lgb_d.partition_broadcast(128), "lgb"), "d_c4")
    mset(stg[:], 0.0)
    mset(stg2[:], 0.0)
    mset(xh[:], 0.0)
    dma("act", View(stg.ap[0:31, 0:CW], "sb", stg.off, stg.off + CW * 4), dview(conv_w_d, "cvw"), "d_c5")
    dma("act", View(stg.ap[31:32, 0:CW], "sb", stg.off, stg.off + CW * 4), dview(conv_b_d, "cvb"), "d_c5")
    dma("act", View(stg.ap[32:33, 0:CW], "sb", stg.off, stg.off + CW * 4), dview(ln_w_d, "lnw"), "d_c5")
    dma("act", View(stg.ap[33:34, 0:CW], "sb", stg.off, stg.off + CW * 4), dview(ln_b_d, "lnb"), "d_c5")
    dma("act", View(stg2.ap[0:NH, :], "sb", stg2.off, stg2.off + 512), dview(rnw_d, "rnw"), "d_c6")
    dma("act", View(stg2.ap[32:32 + KT, :], "sb", stg2.off, stg2.off + 512), dview(n1_d, "n1"), "d_c6")
    dma("act", View(stg2.ap[64:64 + KT, :], "sb", stg2.off, stg2.off + 512), dview(n2_d, "n2"), "d_c6")
    cp(identb[:], identf[:])
    mset(ones_ln[:], 1.0 / CW)
    mset(ones_gn[:], 1.0 / 128)
    for c in range(CT):
        pb = PB[nxt("mm", MM)]
        trp(View(pb.ap[:, 0:34], "ps", pb.off, pb.off + 2048), View(stg.ap[0:34, c * 128:(c + 1) * 128], "sb", stg.off, stg.off + CW * 4),
            View(identf.ap[0:34, 0:34], "sb", identf.off, identf.off + 512))
        cp(cw[:, c, 0:34], pb[:, 0:34])
    for (r0, n, cc) in ((0, NH, c_rnw), (32, KT, c_n1), (64, KT, c_n2)):
        pb = PB[nxt("mm", MM)]
        trp(View(pb.ap[:, 0:n], "ps", pb.off, pb.off + 2048), View(stg2.ap[r0:r0 + n, :], "sb", stg2.off, stg2.off + 512),
            View(identf.ap[r0:r0 + n, r0:r0 + n], "sb", identf.off, identf.off + 512))
        cp(S(cc), pb[:, 0:n])
    sc = 128.0 ** -0.5
    kk.op("pool", lambda e: e.iota(difi.ap, pattern=[[1, 128]], base=0, channel_multiplier=-1), writes=[difi[:]])
    cp(dif[:], difi[:])
    for h in range(NH):
        ts(tmp[0][:, 0:128], dif[:], 0.0, None, ALU.max)
        act(tmp[0][:, 0:128], tmp[0][:, 0:128], AF.Exp, scale=S(c_lgf, h))
        ts(tmp[1][:, 0:128], dif[:], 0.0, sc, ALU.is_ge, ALU.mult)
        tt(tmp[0][:, 0:128], tmp[0][:, 0:128], tmp[1][:, 0:128], ALU.mult)
        ts(tmp[2][:, 0:128], dif[:], -1.0, 0.0, ALU.mult, ALU.max)
        act(tmp[2][:, 0:128], tmp[2][:, 0:128], AF.Exp, scale=S(c_lgb, h))
        ts(tmp[1][:, 0:128], dif[:], 0.0, sc, ALU.is_lt, ALU.mult)
        tt(tmp[2][:, 0:128], tmp[2][:, 0:128], tmp[1][:, 0:128], ALU.mult)
        tt(DcT[:, h, :], tmp[0][:, 0:128], tmp[2][:, 0:128], ALU.add)
    kk.op("pool", lambda e: e.iota(difi.ap, pattern=[[1, 128]], base=1, channel_multiplier=0), writes=[difi[:]])
    cp(dif[:], difi[:])
    for h in range(NH):
        act(XIf[:, h, :], dif[:], AF.Exp, scale=S(c_lgf, h))
    kk.op("pool", lambda e: e.iota(difi.ap, pattern=[[-1, 128]], base=128, channel_multiplier=0), writes=[difi[:]])
    cp(dif[:], difi[:])
    for h in range(NH):
        act(XIb[:, h, :], dif[:], AF.Exp, scale=S(c_lgb, h))
    kk.op("pool", lambda e: e.iota(difi.ap[:, 0:1], pattern=[[0, 1]], base=127, channel_multiplier=-1), writes=[difi[:]])
    cp(S(c_pj), difi[:, 0:1])
    kk.op("pool", lambda e: e.iota(difi.ap[:, 0:1], pattern=[[0, 1]], base=0, channel_multiplier=1), writes=[difi[:]])
    cp(S(c_pj2), difi[:, 0:1])
    act(S(c_zf), S(c_lgf), AF.Exp, scale=S(c_pj))
    ts(S(c_zf), S(c_zf), sc, None, ALU.mult)
    act(S(c_zb), S(c_lgb), AF.Exp, scale=S(c_pj2))
    ts(S(c_zb), S(c_zb), sc, None, ALU.mult)
    act(S(c_cdf), S(c_lgf), AF.Exp, scale=128.0)
    act(S(c_cdb), S(c_lgb), AF.Exp, scale=128.0)

    STOP = cfg.get("stop", 99)

    def finish():
        kk.emit(nc, es, reorder=cfg.get("reorder", True))
        es.close()
        return nc

    if STOP <= 1:
        return finish()
    ncv = [0]

    def conv_unit(dst, src2d, k0, nk, c0, ncol, name):
        src = src2d[k0 * 128:(k0 + nk) * 128, c0:c0 + ncol].rearrange("(k p) c -> p k c", p=128)
        d = dst.rearrange("p (k c) -> p k c", k=nk)
        s = "d_w%d" % (ncv[0] % 4)
        ncv[0] += 1
        kk.op("pool", lambda e: e.dma_start(out=d, in_=src), reads=[dview(None, "src_" + name)], writes=[dview(None, name), dview(None, "semx_" + s)], dma_sem=s,
              dur=2.0 + nk * 128 * ncol * 6 / 150e3)

    def qkv_unit(cc, half):
        return cc * 2 + half

    order = []
    for cc in range(RW // 512, 3 * RW // 512):
        for half in range(2):
            order.append(("qkv", cc, half))
    for u in range(n_cv // 2):
        order.append(("cv", u, 0))
        order.append(("cv", u, 1))
    for cc in range(RW // 512):
        for half in range(2):
            order.append(("qkv", cc, half))
    for u in range(n_g):
        order.append(("g", u, 0))
    for cc in range(D // 512):
        for half in range(2):
            order.append(("wo", cc, half))
    for u in range(n_f1 // 2):
        order.append(("f1", u, 0))
        order.append(("f1", u, 1))
    for cc in range(D // 512):
        for q in range(KQ):
            order.append(("f2", cc, q))
    for (kind, a, b) in order:
        if kind == "qkv":
            u = qkv_unit(a, b)
            conv_unit(s_qkv[u], w_in_d, b * KH, KH, 2 * CW + a * 512, 512, "qkv%d" % u)
        elif kind == "cv":
            conv_unit(s_cv[2 * a + b], w_in_d, 0, KT, b * CW + a * 256, 256, "cv%d" % (2 * a + b))
        elif kind == "g":
            conv_unit(s_g[a], w_in_d, 0, KT, 2 * CW + 3 * RW + a * 256, 256, "g%d" % a)
        elif kind == "wo":
            u = a * 2 + b
            conv_unit(s_wo[u], w_out_d, b * KH, KH, a * 512, 512, "wo%d" % u)
        elif kind == "f1":
            conv_unit(s_f1[2 * a + b], w_f1_d, 0, KT, b * DFF + a * 256, 256, "f1%d" % (2 * a + b))
        elif kind == "f2":
            u = a * KQ + b
            conv_unit(s_f2[u][:, 0:F2K[b] * 512], w_f2_d, b * 8, F2K[b], a * 512, 512, "f2%d" % u)

    if STOP <= 2:
        return finish()

    def wload(src_ap, name, nelem):
        i = nxt("slot", list(range(NSLOT)))
        sl = ring[i]
        dma("sp", sl[:, 0:nelem], dview(src_ap, name), "d_r%d" % i)
        return sl

    def seg_of_chunk(n):
        return n // CPS

    def load_x(b):
        for t in range(4):
            r0 = b * 512 + t * 128
            dma("act", xt[:, t, :], dview(x_d[r0:r0 + 128, :], "x"), "d_x%d" % t)

    def norm_T(b, cn, halo, stream=False, main=True, dstT=None):
        dstT = hT if dstT is None else dstT
        jobs = []
        if main:
            jobs = [((xs[:] if stream else xt[:, t, :]), 128, t) for t in range(4)]
        if halo:
            jobs.append((View(xh.ap[0:32, :], "sb", xh.off, xh.off + D * 4), 32, None))
        for ji, (xv, rows, t) in enumerate(jobs):
            if stream and t is not None:
                r0 = b * 512 + t * 128
                dma("act", xs[:], dview(x_d[r0:r0 + 128, :], "x"), "d_xs")
            hbuf = h_bf if stream else (h_bf, h_bf2)[ji % 2]
            hb = View(hbuf.ap[0:rows, :], "sb", hbuf.off, hbuf.off + D * 2)
            ti = 4 if t is None else t
            ssv = View(small.ap[0:rows, c_ss[0] + ti:c_ss[0] + ti + 1], "sb", small.off, small.off + 512)
            rsv = View(small.ap[0:rows, c_rs[0] + ti:c_rs[0] + ti + 1], "sb", small.off, small.off + 512)
            mset(ssv, 0.0)
            act(hb, xv, AF.Square, accum=ssv)
            act(rsv, ssv, AF.Sqrt, scale=1.0 / D, bias=EPS)
            recip(rsv, rsv)
            if stream:
                act(hb, xv, AF.Copy, scale=rsv)
            else:
                ts(hb, xv, rsv, None, ALU.mult)
            for kg in range(KT // 4):
                pb = PBb[nxt("tr", TR)]
                for j in range(4):
                    k = kg * 4 + j
                    trp(View(pb.ap[:, j, 0:rows], "ps", pb.off, pb.off + 1024),
                        View(hbuf.ap[0:rows, k * 128:(k + 1) * 128], "sb", hbuf.off + k * 256, hbuf.off + (k + 1) * 256),
                        View(identb.ap[0:rows, 0:rows], "sb", identb.off, identb.off + 256))
                for j in range(4):
                    k = kg * 4 + j
                    dst = dstT[:, k, t * 128:(t + 1) * 128] if t is not None else hTh[:, k, :]
                    act(dst, View(pb.ap[:, j, 0:rows], "ps", pb.off, pb.off + 1024), AF.Copy, scale=S(cn, k))

    def tm_stage(units, lhs_fn, evac_fn):
        nsplit = len(units)
        k0 = 0
        for si, (src, name, nk) in enumerate(units):
            sl = wload(src, name, nk * 512)
            slv = Buf(sl.ap[:, 0:nk * 512].rearrange("p (k c) -> p k c", k=nk), "sb", sl.off, [128, nk, 512], 2)
            for t in range(4):
                for k in range(nk):
                    mm(PB[MM[t]][:, :], lhs_fn(k0 + k, t), slv[:, k, :], start=(si == 0 and k == 0), stop=(si == nsplit - 1 and k == nk - 1))
            k0 += nk
        for t in range(4):
            evac_fn(t, MM[t])

    def rotary(dst, t, h0, pbi):
        P4 = PB4[pbi]
        x1 = P4[:, :, 0:64]
        x2 = P4[:, :, 64:128]
        cb = bc(cos4[:, t, :], [128, 4, 64], 1)
        sb_ = bc(sin4[:, t, :], [128, 4, 64], 1)
        tt(rt[0][:], x1, cb, ALU.mult)
        tt(rt[1][:], x2, sb_, ALU.mult)
        tt(dst[:, t, h0:h0 + 4, 0:64], rt[0][:], rt[1][:], ALU.subtract)
        tt(rt[0][:], x1, sb_, ALU.mult)
        tt(rt[1][:], x2, cb, ALU.mult)
        tt(dst[:, t, h0:h0 + 4, 64:128], rt[0][:], rt[1][:], ALU.add)

    def load_cs(b):
        dma("act", cos4[:], dview(cos_d[b * 4:b * 4 + 4].rearrange("c p f -> p c f"), "cos"), "d_cos")
        dma("act", sin4[:], dview(sin_d[b * 4:b * 4 + 4].rearrange("c p f -> p c f"), "sin"), "d_sin")

    def qkv_proj(b, which, srcT=None):
        srcT = hT if srcT is None else srcT
        nq = RW // 512
        for cc in which:
            units = [(s_qkv[cc * 2 + half], "qkv%d" % (cc * 2 + half), KH) for half in range(2)]
            kind = cc // nq
            h0 = (cc % nq) * 4

            def evac(t, pbi, kind=kind, h0=h0):
                if kind == 0:
                    rotary(q_rot, t, h0, pbi)
                elif kind == 1:
                    rotary(k_rot, t, h0, pbi)
                else:
                    act(v_sb[:, t, h0:h0 + 4, :], PB4[pbi][:], AF.Copy)

            tm_stage(units, lambda k, t: srcT[:, k, t * 128:(t + 1) * 128], evac)

    for b in range(NB - 1, -1, -1):
        load_x(b)
        load_cs(b)
        hTp = (hT, hT2)[b % 2]
        norm_T(b, c_n1, False, dstT=hTp)
        qkv_proj(b, list(range(RW // 512, 3 * RW // 512)), srcT=hTp)
        dma("act", dview(kvs_d[b, 0], "kvk%d" % b), k_rot[:], "d_kvk")
        dma("act", dview(kvs_d[b, 1], "kvv%d" % b), v_sb[:], "d_kvv")
        for t in range(3, -1, -1):
            n = b * 4 + t
            if n % CPS == CPS - 1:
                if seg_of_chunk(n) == 0 and n != NCH - 1:
                    ts(Rb32[:], Rb32[:], S(c_link), None, ALU.mult)
                else:
                    mset(Rb32[:], 0.0)
            st = Rb_bf[n % 2]
            cp(st[:], Rb32[:], eng="act")
            dma("act", dview(rbs_d[n], "rb%d" % n), st[:], "d_rb%d" % (n % 2))
            for hg in range(NHG):
                hs = slice(hg * 4, hg * 4 + 4)
                kz = kzs[(n * NHG + hg) % 2]
                for j in range(4):
                    act(kz[:, j, :], k_rot[:, t, hg * 4 + j, :], AF.Copy, scale=S(c_zb, hg * 4 + j))
                pbi = nxt("mm", MM)
                for j in range(4):
                    mm(PB4[pbi][:, j, :], kz[:, j, :], v_sb[:, t, hg * 4 + j, :])
                for j in range(4):
                    h = hg * 4 + j
                    stt(Rb32[:, h, :], Rb32[:, h, :], S(c_cdb, h), PB4[pbi][:, j, :], ALU.mult, ALU.add)

    if STOP <= 3:
        return finish()
    for b in range(NB):
        if STOP == 4 and b == 1:
            return finish()
        seg = b // SEGB
        first_in_seg = (b % SEGB == 0)
        last_in_seg = (b % SEGB == SEGB - 1)
        if first_in_seg:
            lmode = "link" if seg == 1 else "zero"
        else:
            lmode = "real"
        if last_in_seg:
            rmode = "link" if seg == 0 and cfg["NSEG"] > 1 else "zero"
        else:
            rmode = "real"
        halo = (lmode != "zero") or (rmode != "zero")
        if b == 0:
            load_cs(0)
            norm_T(0, c_n1, False, stream=True)
        if halo:
            if lmode != "zero":
                dma("act", View(xh.ap[0:16, :], "sb", xh.off, xh.off + D * 4), dview(x_d[b * 512 - 16:b * 512, :], "x"), "d_xh")
            if rmode != "zero":
                dma("act", View(xh.ap[16:32, :], "sb", xh.off, xh.off + D * 4), dview(x_d[b * 512 + 512:b * 512 + 528, :], "x"), "d_xh")
        if halo:
            norm_T(b, c_n1, True, main=False)
        if STOP == 41:
            return finish()
        for u in range(n_cv // 2):
            sa = wload(s_cv[2 * u], "cv%d" % (2 * u), KT * 256)
            sg_ = wload(s_cv[2 * u + 1], "cv%d" % (2 * u + 1), KT * 256)
            sav = Buf(sa.ap[:, 0:KT * 256].rearrange("p (k c) -> p k c", k=KT), "sb", sa.off, [128, KT, 256], 2)
            sgv = Buf(sg_.ap[:, 0:KT * 256].rearrange("p (k c) -> p k c", k=KT), "sb", sg_.off, [128, KT, 256], 2)
            for m_ in range(2):
                c = 2 * u + m_
                pa = nxt("mm", MM)
                pg = nxt("mm", MM)
                for k in range(KT):
                    mm(PB[pa][:, :], sav[:, k, m_ * 128:(m_ + 1) * 128], hT[:, k, :], start=(k == 0), stop=(k == KT - 1))
                for k in range(KT):
                    mm(PB[pg][:, :], sgv[:, k, m_ * 128:(m_ + 1) * 128], hT[:, k, :], start=(k == 0), stop=(k == KT - 1))
                if halo:
                    for k in range(KT):
                        mm(PB[RT0][:, 0:32], sav[:, k, m_ * 128:(m_ + 1) * 128], hTh[:, k, :], start=(k == 0), stop=(k == KT - 1))
                    for k in range(KT):
                        mm(PB[RT0][:, 128:160], sgv[:, k, m_ * 128:(m_ + 1) * 128], hTh[:, k, :], start=(k == 0), stop=(k == KT - 1))
                tp = tmp[nxt("tmp", [0, 1])]
                act(tp[:], PB[pg][:, :], AF.Sigmoid)
                tt(glu[:, c, 15:527], PB[pa][:, :], tp[:], ALU.mult)
                if halo:
                    act(tmp[2][:, 0:32], PB[RT0][:, 128:160], AF.Sigmoid)
                for (mode, dst, sl_) in ((lmode, glu[:, c, 0:15], slice(1, 16)), (rmode, glu[:, c, 527:542], slice(16, 31))):
                    if mode == "zero":
                        mset(dst, 0.0)
                    elif mode == "real":
                        tt(dst, PB[RT0][:, sl_], tmp[2][:, sl_], ALU.mult)
                    else:
                        stt(dst, PB[RT0][:, sl_], S(c_link), tmp[2][:, sl_], ALU.mult, ALU.mult)
        def conv_pairs(p0, p1):
            for c0 in range(2 * p0, min(2 * p1, CT), 2):
                for j in range(31):
                    for c in range(c0, min(c0 + 2, CT)):
                        if j == 0:
                            ts(acc[:, c, :], glu[:, c, 0:512], cw[:, c, 0:1], cw[:, c, 31:32], ALU.mult, ALU.add)
                        else:
                            stt(acc[:, c, :], glu[:, c, j:j + 512], cw[:, c, j:j + 1], acc[:, c, :], ALU.mult, ALU.add)

        NP_EARLY = cfg.get("np_early", 2)
        conv_pairs(0, NP_EARLY)
        if STOP == 44:
            return finish()
        dma("act", k_rot[:], dview(kvs_d[b, 0], "kvk%d" % b), "d_kvk")
        dma("act", v_sb[:], dview(kvs_d[b, 1], "kvv%d" % b), "d_kvv")
        qkv_proj(b, list(range(RW // 512)))
        if STOP == 45:
            return finish()
        for u in range(n_g):
            sl = wload(s_g[u], "g%d" % u, KT * 256)
            slv = Buf(sl.ap[:, 0:KT * 256].rearrange("p (k c) -> p k c", k=KT), "sb", sl.off, [128, KT, 256], 2)
            for m_ in range(2):
                h = 2 * u + m_
                pg = nxt("mm", MM)
                for k in range(KT):
                    mm(PB[pg][:, :], slv[:, k, m_ * 128:(m_ + 1) * 128], hT[:, k, :], start=(k == 0), stop=(k == KT - 1))
                tp = tmp[nxt("tmp", [0, 1])]
                act(tp[:], PB[pg][:, :], AF.Silu)
                act(sgT[:, h, :], tp[:], AF.Copy, scale=S(c_rnw, h))
        if STOP == 46:
            return finish()
        for t in range(4):
            n = b * 4 + t
            rbuf = Rb_bf[n % 2]
            dma("act", rbuf[:], dview(rbs_d[n], "rb%d" % n), "d_rb%d" % (n % 2))
            if n % CPS == 0:
                if seg_of_chunk(n) == 1:
                    ts(Rf32[:], Rf32[:], S(c_link), None, ALU.mult)
                else:
                    mset(Rf32[:], 0.0)
                cp(Rf_bf[:], Rf32[:], eng="act")
            for hg in range(NHG):
                hs = slice(hg * 4, hg * 4 + 4)
                par = (n * NHG + hg) % 2
                PT = PTs[par]
                kz = kzs[par]
                tA, tB = tmp[2 * par], tmp[2 * par + 1]
                bS = (RT0, RT1)[par]
                bO = MM[par]
                bX = MM[2 + par]
                pq = PBb[nxt("tr", TR)]
                for j in range(4):
                    trp(pq[:, j, :], q_rot[:, t, hg * 4 + j, :], identb[:])
                pk = PBb[nxt("tr", TR)]
                for j in range(4):
                    trp(pk[:, j, :], k_rot[:, t, hg * 4 + j, :], identb[:])
                if STOP == 4601:
                    return finish()
                act(qT[:, hs, :], pq[:, 0:4, :], AF.Copy)
                tt(qfT[:, hs, :], qT[:, hs, :], XIf[:, hs, :], ALU.mult)
                tt(qbT[:, hs, :], qT[:, hs, :], XIb[:, hs, :], ALU.mult)
                act(kT[:, hs, :], pk[:, 0:4, :], AF.Copy)
                for j in range(4):
                    mm(PB4[bS][:, j, :], kT[:, hg * 4 + j, :], qT[:, hg * 4 + j, :])
                tt(PT[:], PB4[bS][:], DcT[:, hs, :], ALU.mult)
                for j in range(4):
                    h = hg * 4 + j
                    mm(PB4[bO][:, j, :], v_sb[:, t, h, :], PT[:, j, :], start=True, stop=False)
                    mm(PB4[bO][:, j, :], Rf_bf[:, h, :], qfT[:, h, :], start=False, stop=False)
                    mm(PB4[bO][:, j, :], rbuf[:, h, :], qbT[:, h, :], start=False, stop=True)
                act(tA[:], PB[bO][:, :], AF.Square)
                mm(PB[bX][:, :], ones_gn[:], tA[:])
                act(tB[:], PB[bX][:, :], AF.Sqrt, bias=EPS)
                recip_big(tB[:], tB[:], tA[:])
                tt(tA[:], PB[bO][:, :], tB[:], ALU.mult)
                tAv = View(tA.ap.rearrange("p (a b) -> p a b", a=4), "sb", tA.off, tA.off + 2048)
                tt(ymix[:, CT + hg * 4:CT + hg * 4 + 4, t * 128:(t + 1) * 128], tAv, sgT[:, hs, t * 128:(t + 1) * 128], ALU.mult)
                for j in range(4):
                    act(kz[:, j, :], k_rot[:, t, hg * 4 + j, :], AF.Copy, scale=S(c_zf, hg * 4 + j))
                for j in range(4):
                    mm(PB4[bX][:, j, :], kz[:, j, :], v_sb[:, t, hg * 4 + j, :])
                for j in range(4):
                    h = hg * 4 + j
                    stt(Rf32[:, h, :], Rf32[:, h, :], S(c_cdf, h), PB4[bX][:, j, :], ALU.mult, ALU.add)
                cp(Rf_bf[:, hs, :], Rf32[:, hs, :], eng="act")
        if STOP == 42:
            return finish()
        conv_pairs(NP_EARLY, (CT + 1) // 2)
        if STOP == 43:
            return finish()
        pm = nxt("mm", MM)
        pe2 = nxt("mm", MM)
        for c in range(CT):
            tp = tmp[nxt("tmp", [0, 1])]
            act(tp[:], acc[:, c, :], AF.Square)
            mm(PB[pm][:, :], ones_ln[:], acc[:, c, :], start=(c == 0), stop=(c == CT - 1))
            mm(PB[pe2][:, :], ones_ln[:], tp[:], start=(c == 0), stop=(c == CT - 1))
        act(tmp[2][:], PB[pm][:, :], AF.Copy)
        tt(tmp[3][:], tmp[2][:], tmp[2][:], ALU.mult)
        tt(tmp[3][:], PB[pe2][:, :], tmp[3][:], ALU.subtract)
        act(tmp[0][:], tmp[3][:], AF.Sqrt, bias=EPS)
        recip_big(tmp[3][:], tmp[0][:], tmp[1][:])
        for c in range(CT):
            tp = tmp[nxt("tmp", [0, 1])]
            tt(tp[:], acc[:, c, :], tmp[2][:], ALU.subtract)
            tt(tp[:], tp[:], tmp[3][:], ALU.mult)
            act(ymix[:, c, :], tp[:], AF.Silu, scale=cw[:, c, 32:33], bias=cw[:, c, 33:34])
        if STOP == 47:
            return finish()
        load_x(b)
        for cc in range(D // 512):
            units = [(s_wo[cc * 2 + half], "wo%d" % (cc * 2 + half), KH) for half in range(2)]

            def evac9(t, pbi, cc=cc):
                tt(xt[:, t, cc * 512:(cc + 1) * 512], PB[pbi][:, :], xt[:, t, cc * 512:(cc + 1) * 512], ALU.add)

            tm_stage(units, lambda k, t: ymix[:, k, t * 128:(t + 1) * 128], evac9)
        if STOP == 48:
            return finish()
        norm_T(b, c_n2, False)
        if STOP == 49:
            return finish()
        for u in range(n_f1 // 2):
            sa = wload(s_f1[2 * u], "f1%d" % (2 * u), KT * 256)
            sb2 = wload(s_f1[2 * u + 1], "f1%d" % (2 * u + 1), KT * 256)
            sav = Buf(sa.ap[:, 0:KT * 256].rearrange("p (k c) -> p k c", k=KT), "sb", sa.off, [128, KT, 256], 2)
            sbv = Buf(sb2.ap[:, 0:KT * 256].rearrange("p (k c) -> p k c", k=KT), "sb", sb2.off, [128, KT, 256], 2)
            for m_ in range(2):
                f = 2 * u + m_
                pg = nxt("mm", MM)
                pu = nxt("mm", MM)
                for k in range(KT):
                    mm(PB[pg][:, :], sav[:, k, m_ * 128:(m_ + 1) * 128], hT[:, k, :], start=(k == 0), stop=(k == KT - 1))
                for k in range(KT):
                    mm(PB[pu][:, :], sbv[:, k, m_ * 128:(m_ + 1) * 128], hT[:, k, :], start=(k == 0), stop=(k == KT - 1))
                tp = tmp[nxt("tmp", [0, 1])]
                act(tp[:], PB[pg][:, :], AF.Silu)
                tt(actT[:, f, :], PB[pu][:, :], tp[:], ALU.mult)
        if STOP == 50:
            return finish()
        if b + 1 < NB:
            load_cs(b + 1)
            norm_T(b + 1, c_n1, False, stream=True)
        for cc in range(D // 512):
            units = [(s_f2[cc * KQ + q][:, 0:F2K[q] * 512], "f2%d" % (cc * KQ + q), F2K[q]) for q in range(KQ)]

            def evac12(t, pbi, cc=cc):
                tt(xt[:, t, cc * 512:(cc + 1) * 512], PB[pbi][:, :], xt[:, t, cc * 512:(cc + 1) * 512], ALU.add)

            tm_stage(units, lambda k, t: actT[:, k, t * 128:(t + 1) * 128], evac12)
        if STOP == 51:
            return finish()
        for t in range(4):
            mset(S(c_ss, t), 0.0)
            act(h_bf[:], xt[:, t, :], AF.Square, accum=S(c_ss, t))
            ts(S(c_rs, t), S(c_ss, t), 1.0 / D, EPS, ALU.mult, ALU.add)
            act(S(c_rs, t), S(c_rs, t), AF.Sqrt)
            recip(S(c_rs, t), S(c_rs, t))
            r0 = b * 512 + t * 128
            for hf in range(2):
                cs_ = slice(hf * (D // 2), (hf + 1) * (D // 2))
                stt(xs[:, cs_], xt[:, t, cs_], S(c_rs, t), wfb[:, cs_], ALU.mult, ALU.mult)
                dma("act", dview(y_d[r0:r0 + 128, cs_], "y%d_%d" % (r0, hf)), xs[:, cs_], "d_ys%d" % hf)

    return finish()


def rope_tables(positions):
    inv_freq = (10000.0 ** (-np.arange(0, 128, 2, dtype=np.float32) / np.float32(128))).astype(np.float32)
    ang = positions.astype(np.float32)[:, None] * inv_freq[None, :]
    return np.cos(ang).astype(np.float32), np.sin(ang).astype(np.float32)


def core_layout(cfg, n_prompt, n_sample, seq_p):
    out = []
    for c in range(NCORES):
        if c < n_sample:
            segs = [("s", c, 0), ("s", c, seq_p), ("p", c, 0)]
            link = 1.0
        else:
            j = c - n_sample
            base = n_sample + 3 * j
            segs = [("p", base, 0), ("p", base + 1, 0), ("p", base + 2, 0)]
            link = 0.0
        out.append((segs, link))
    return out


_NC_CACHE = {}


def kernel(x_prompt, x_sample, norm1_w, w_in, conv_w, conv_b, conv_ln_w, conv_ln_b,
           ret_log_decay_fwd, ret_log_decay_bwd, ret_norm_w, w_out,
           norm2_w, w_ffn_in, w_ffn_out, final_norm_w):
    x_prompt = np.asarray(x_prompt, dtype=np.float32)
    x_sample = np.asarray(x_sample, dtype=np.float32)
    Bp, Sp, D = x_prompt.shape
    Bs, Ss, _ = x_sample.shape
    DFF = np.asarray(w_ffn_out).shape[1]
    assert Ss == 2 * Sp and Bs * 2 + Bp == 3 * NCORES
    cfg = make_cfg(D=D, DFF=DFF, SEGB=Sp // 512, NSEG=3)
    key = (D, DFF, Sp)
    if key not in _NC_CACHE:
        _NC_CACHE[key] = build_nc(cfg)
    nc = _NC_CACHE[key]
    lay = core_layout(cfg, Bp, Bs, Sp)
    f = lambda a: np.ascontiguousarray(np.asarray(a, dtype=np.float32))
    shared = {
        "w_in": f(w_in)[0], "w_out": f(w_out)[0], "w_ffn_in": f(w_ffn_in)[0], "w_ffn_out": f(w_ffn_out)[0],
        "conv_w": f(conv_w)[0], "conv_b": f(conv_b).reshape(1, -1), "conv_ln_w": f(conv_ln_w).reshape(1, -1),
        "conv_ln_b": f(conv_ln_b).reshape(1, -1), "lgf": f(ret_log_decay_fwd).reshape(-1), "lgb": f(ret_log_decay_bwd).reshape(-1),
        "ret_norm_w": f(ret_norm_w).reshape(-1, 128), "norm1_w": f(norm1_w).reshape(-1, 128), "norm2_w": f(norm2_w).reshape(-1, 128),
        "final_norm_w": f(final_norm_w).reshape(-1), "ident": np.eye(128, dtype=np.float32),
    }
    in_maps = []
    for c in range(NCORES):
        segs, link = lay[c]
        xs, ps = [], []
        for (which, bi, r0) in segs:
            src = x_sample if which == "s" else x_prompt
            xs.append(src[bi, r0:r0 + Sp])
            ps.append(np.arange(r0, r0 + Sp))
        cos, sin = rope_tables(np.concatenate(ps))
        m = dict(shared)
        m["x"] = np.ascontiguousarray(np.concatenate(xs, axis=0))
        m["link"] = np.full((128, 1), link, dtype=np.float32)
        m["cos_t"] = np.ascontiguousarray(cos.reshape(-1, 128, 64))
        m["sin_t"] = np.ascontiguousarray(sin.reshape(-1, 128, 64))
        in_maps.append(m)
    res = run_bass_kernel_spmd(nc, in_maps, core_ids=list(range(NCORES)))
    y_prompt = np.empty_like(x_prompt)
    y_sample = np.empty_like(x_sample)
    for c in range(NCORES):
        segs, _ = lay[c]
        yc = res.results[c]["y"]
        for i, (which, bi, r0) in enumerate(segs):
            dst = y_sample if which == "s" else y_prompt
            dst[bi, r0:r0 + Sp] = yc[i * Sp:(i + 1) * Sp]
    return (y_prompt, y_sample)
```

```python
import math
from contextlib import ExitStack

import numpy as np
import concourse.bass as bass
import concourse.mybir as mybir
from concourse.bass_utils import run_bass_kernel_spmd

F32 = mybir.dt.float32
BF16 = mybir.dt.bfloat16
I32 = mybir.dt.int32
AF = mybir.ActivationFunctionType
ALU = mybir.AluOpType

EPS = 1e-6
PG = 512
NCORES = 8


class View:
    __slots__ = ("ap", "space", "lo", "hi")

    def __init__(self, ap, space, lo, hi):
        self.ap, self.space, self.lo, self.hi = ap, space, lo, hi

    def m(self, fn):
        return View(fn(self.ap), self.space, self.lo, self.hi)


class Buf:
    def __init__(self, ap, space, off, shape, esz):
        self.ap, self.space, self.off, self.shape, self.esz = ap, space, off, tuple(shape), esz
        st = [1]
        for d in reversed(self.shape[2:]):
            st.insert(0, st[0] * d)
        self.strides = st

    def __getitem__(self, key):
        if not isinstance(key, tuple):
            key = (key,)
        key = key + (slice(None),) * (len(self.shape) - len(key))
        lo = hi = 0
        for k, n, s in zip(key[1:], self.shape[1:], self.strides):
            if isinstance(k, slice):
                a = 0 if k.start is None else k.start
                b = n if k.stop is None else k.stop
            else:
                a, b = k, k + 1
            assert 0 <= a < b <= n, (key, self.shape)
            lo += a * s
            hi += (b - 1) * s
        return View(self.ap[key], self.space, self.off + lo * self.esz, self.off + (hi + 1) * self.esz)


def dview(ap, name):
    return View(ap, "dr:" + name, 0, 1)


cfg_lat = [1.2]


class K:
    ENG = ("pe", "act", "dve", "pool", "sp")

    def __init__(self):
        self.ops = []
        self.lw = {}
        self.rd = {}
        self.nops = 0

    @staticmethod
    def pages(v):
        if v.space.startswith("dr"):
            return [(v.space, 0)]
        return [(v.space, p) for p in range(v.lo // PG, (v.hi - 1) // PG + 1)]

    def op(self, eng, fn, reads=(), writes=(), dma_sem=None, dur=0.3):
        deps = set()
        for v in reads:
            for p in self.pages(v):
                w = self.lw.get(p)
                if w is not None:
                    deps.add(w)
        for v in writes:
            for p in self.pages(v):
                w = self.lw.get(p)
                if w is not None:
                    deps.add(w)
                r = self.rd.get(p)
                if r:
                    deps.update(r)
        i = len(self.ops)
        self.ops.append((eng, fn, tuple(deps), dma_sem, dur))
        self.nops += 1
        for v in reads:
            for p in self.pages(v):
                self.rd.setdefault(p, set()).add(i)
        for v in writes:
            for p in self.pages(v):
                self.lw[p] = i
                self.rd[p] = set()

    def schedule(self, reorder=True):
        import heapq
        ops = self.ops
        n = len(ops)
        order = {e: [] for e in self.ENG}
        if not reorder:
            for i, o in enumerate(ops):
                order[o[0]].append(i)
            return order
        indeg = [len(o[2]) for o in ops]
        users = [[] for _ in range(n)]
        for i, o in enumerate(ops):
            for d in o[2]:
                users[d].append(i)
        ready = {e: [] for e in self.ENG}
        busy = {e: False for e in self.ENG}
        ev = []
        seq = [0]

        def push(t, kind, i):
            seq[0] += 1
            heapq.heappush(ev, (t, seq[0], kind, i))

        LAT = cfg_lat[0]
        for i in range(n):
            if indeg[i] == 0:
                heapq.heappush(ready[ops[i][0]], i)

        def try_start(e, now):
            if busy[e] or not ready[e]:
                return
            i = heapq.heappop(ready[e])
            busy[e] = True
            order[e].append(i)
            o = ops[i]
            if o[3] is not None:
                push(now + 0.06, 1, i)
                push(now + o[4], 0, i)
            else:
                push(now + o[4], 0, i)

        for e in self.ENG:
            try_start(e, 0.0)
        while ev:
            t, _, kind, i = heapq.heappop(ev)
            e = ops[i][0]
            if kind == 2:
                heapq.heappush(ready[e], i)
                try_start(e, t)
                continue
            if kind == 1:
                busy[e] = False
                try_start(e, t)
                continue
            if ops[i][3] is None:
                busy[e] = False
            for u in users[i]:
                indeg[u] -= 1
                if indeg[u] == 0:
                    push(t + (0.05 if ops[u][0] == e else LAT), 2, u)
            try_start(e, t)
        assert sum(len(v) for v in order.values()) == n, "scheduler lost ops (cycle?)"
        self.sim_time = t
        return order

    def emit(self, nc, es, reorder=True):
        ops = self.ops
        order = self.schedule(reorder)
        semval = [None] * len(ops)
        cnt = {}
        for e in self.ENG:
            for i in order[e]:
                o = ops[i]
                if o[3] is not None:
                    sn, inc = o[3], 16
                else:
                    sn, inc = "e_" + e, 1
                cnt[sn] = cnt.get(sn, 0) + inc
                semval[i] = (sn, cnt[sn], inc)
        sems = {sn: es.enter_context(nc.semaphore(sn)) for sn in cnt}
        streams = {}
        for e in self.ENG:
            own = "e_" + e
            waited = {}
            items = []
            for i in order[e]:
                need = {}
                for d in ops[i][2]:
                    sn, val, _ = semval[d]
                    if sn == own and e == "pe":
                        continue
                    if need.get(sn, 0) < val:
                        need[sn] = val
                for sn, val in need.items():
                    if waited.get(sn, 0) >= val:
                        continue
                    waited[sn] = val
                    items.append(("w", sn, val))
                items.append(("o", ops[i][1], semval[i][0], semval[i][2]))
            streams[e] = items
        for sn, val in cnt.items():
            if not sn.startswith("e_"):
                streams["sp"].append(("w", sn, val))
        block = es.enter_context(nc.Block())

        def runner(items):
            def run(eng):
                for it in items:
                    if it[0] == "w":
                        eng.wait_ge(sems[it[1]], it[2])
                    else:
                        it[1](eng).then_inc(sems[it[2]], it[3])
            return run

        block.tensor(runner(streams["pe"]))
        block.scalar(runner(streams["act"]))
        block.vector(runner(streams["dve"]))
        block.gpsimd(runner(streams["pool"]))
        block.sync(runner(streams["sp"]))


def make_cfg(D=2048, DFF=5632, SEGB=4, NSEG=3):
    c = dict(D=D, DFF=DFF, SEGB=SEGB, NSEG=NSEG)
    c["KT"] = D // 128
    c["CW"] = D // 2
    c["RW"] = D // 2
    c["CT"] = c["CW"] // 128
    c["NH"] = c["RW"] // 128
    c["FT"] = DFF // 128
    c["NB"] = SEGB * NSEG
    c["NCH"] = c["NB"] * 4
    c["NTOK"] = c["NB"] * 512
    c["INC"] = 2 * c["CW"] + 4 * c["RW"]
    return c


def build_nc(cfg):
    D, DFF, KT, CW, RW, CT, NH, FT = (cfg[k] for k in ("D", "DFF", "KT", "CW", "RW", "CT", "NH", "FT"))
    NB, NCH, NTOK, SEGB, INC = (cfg[k] for k in ("NB", "NCH", "NTOK", "SEGB", "INC"))
    KH = KT // 2
    assert RW % 512 == 0 and D % 512 == 0 and CT % 2 == 0 and NH % 4 == 0 and KT <= 16
    NHG = NH // 4
    F2K = [min(8, FT - q * 8) for q in range((FT + 7) // 8)]
    KQ = len(F2K)
    CPS = SEGB * 4
    nc = bass.Bass("TRN2", target_bir_lowering=False)
    kk = K()
    es = ExitStack()

    def din(name, shape, dt=F32):
        return nc.dram_tensor(name, list(shape), dt, kind="ExternalInput").ap()

    x_d = din("x", [NTOK, D])
    w_in_d = din("w_in", [D, INC])
    w_out_d = din("w_out", [D, D])
    w_f1_d = din("w_ffn_in", [D, 2 * DFF])
    w_f2_d = din("w_ffn_out", [DFF, D])
    conv_w_d = din("conv_w", [31, CW])
    conv_b_d = din("conv_b", [1, CW])
    ln_w_d = din("conv_ln_w", [1, CW])
    ln_b_d = din("conv_ln_b", [1, CW])
    lgf_d = din("lgf", [NH])
    lgb_d = din("lgb", [NH])
    rnw_d = din("ret_norm_w", [NH, 128])
    n1_d = din("norm1_w", [KT, 128])
    n2_d = din("norm2_w", [KT, 128])
    nf_d = din("final_norm_w", [D])
    ident_d = din("ident", [128, 128])
    link_d = din("link", [128, 1])
    cos_d = din("cos_t", [NCH, 128, 64])
    sin_d = din("sin_t", [NCH, 128, 64])
    y_d = nc.dram_tensor("y", [NTOK, D], F32, kind="ExternalOutput").ap()

    def dscr(name, n, elems):
        return nc.dram_tensor(name, [n, 128, elems], BF16, kind="Internal").ap()

    n_cv = 2 * CW // 256
    n_qkv = (3 * RW // 512) * 2
    n_g = RW // 256
    n_wo = (D // 512) * 2
    n_f1 = 2 * DFF // 256
    n_f2 = (D // 512) * KQ
    s_cv = dscr("s_cv", n_cv, KT * 256)
    s_qkv = dscr("s_qkv", n_qkv, KH * 512)
    s_g = dscr("s_g", n_g, KT * 256)
    s_wo = dscr("s_wo", n_wo, KH * 512)
    s_f1 = dscr("s_f1", n_f1, KT * 256)
    s_f2 = dscr("s_f2", n_f2, 8 * 512)
    rbs_d = nc.dram_tensor("s_rb", [NCH, 128, NH * 128], BF16, kind="Internal").ap()
    kvs_d = nc.dram_tensor("s_kv", [NB, 2, 128, 4 * RW], BF16, kind="Internal").ap()

    pos = [0]
    base_t = es.enter_context(nc.sbuf_tensor("arena", [128, 206 * 1024], mybir.dt.uint8))
    arena_off = nc.lookup_mloc(base_t).addr
    cnt_alloc = [0]

    arena_ap = base_t.ap()

    def alloc_at(off, shape, dt):
        esz = 2 if dt == BF16 else 4
        n = 1
        for d in shape[1:]:
            n *= d
        ap = arena_ap[:, off:off + n * esz].bitcast(dt)
        if len(shape) == 3:
            ap = ap.rearrange("p (a b) -> p a b", a=shape[1])
        elif len(shape) == 4:
            ap = ap.rearrange("p (a b c) -> p a b c", a=shape[1], b=shape[2])
        return Buf(ap, "sb", off, shape, esz)

    def alloc(shape, dt, align=PG):
        esz = 2 if dt == BF16 else 4
        n = 1
        for d in shape[1:]:
            n *= d
        off = (pos[0] + align - 1) // align * align
        pos[0] = off + n * esz
        assert pos[0] <= 206 * 1024, "SBUF arena overflow %d" % pos[0]
        return alloc_at(off, shape, dt)

    DcT = alloc([128, NH, 128], F32)
    XIf = alloc([128, NH, 128], F32)
    XIb = alloc([128, NH, 128], F32)
    wfb = alloc([128, D], F32)
    identf = alloc([128, 128], F32)
    identb = alloc([128, 128], BF16)
    ones_ln = alloc([128, 128], F32)
    ones_gn = alloc([128, 128], F32)
    cw = alloc([128, CT, 36], F32)
    small = alloc([128, 128], F32)
    o = [0]

    def sm(n):
        v = (o[0], o[0] + n)
        o[0] += n
        assert o[0] <= 128
        return v

    c_rnw, c_n1, c_n2, c_lgf, c_lgb, c_zf, c_zb, c_cdf, c_cdb = (sm(NH), sm(KT), sm(KT), sm(NH), sm(NH), sm(NH), sm(NH), sm(NH), sm(NH))
    c_link, c_pj, c_pj2, c_ss, c_rs = sm(1), sm(1), sm(1), sm(5), sm(5)

    def S(c, i=None):
        if i is None:
            return small[:, c[0]:c[1]]
        return small[:, c[0] + i:c[0] + i + 1]

    Rf32 = alloc([128, NH, 128], F32)
    Rf_bf = alloc([128, NH, 128], BF16)
    Rb_bf = [alloc([128, NH, 128], BF16) for _ in range(2)]
    SLOT = 8 * 512 * 2
    assert KT * 256 * 2 <= SLOT and KH * 512 * 2 <= SLOT
    NSLOT = 4
    ring = [alloc([128, SLOT // 2], BF16) for _ in range(NSLOT)]
    cos4 = alloc([128, 4, 64], F32)
    sin4 = alloc([128, 4, 64], F32)
    tmp = [alloc([128, 512], F32) for _ in range(4)]
    rt = [alloc_at(tmp[3].off + i * 1024, [128, 4, 64], F32) for i in range(2)]
    h_bf = alloc([128, D], BF16)
    xs = alloc([128, D], F32)
    hT = alloc([128, KT, 512], BF16)
    hTh = alloc([128, KT, 32], BF16)
    qT = alloc([128, NH, 128], BF16)
    qfT = alloc([128, NH, 128], BF16)
    qbT = alloc([128, NH, 128], BF16)
    kT = alloc([128, NH, 128], BF16)
    PTs = [alloc([128, 4, 128], BF16) for _ in range(2)]
    kzs = [alloc([128, 4, 128], BF16) for _ in range(2)]
    xr_off = (pos[0] + PG - 1) // PG * PG
    xr_size = max(4 * D * 4, CT * 542 * 4 + CT * 512 * 4 + PG)
    pos[0] = xr_off + xr_size
    xt = alloc_at(xr_off, [128, 4, D], F32)
    glu = alloc_at(xr_off, [128, CT, 542], F32)
    acc_off = (xr_off + CT * 542 * 4 + PG - 1) // PG * PG
    acc = alloc_at(acc_off, [128, CT, 512], F32)
    q_rot = alloc([128, 4, NH, 128], BF16)
    m_off = (pos[0] + PG - 1) // PG * PG
    ymix_sz = KT * 512 * 2
    sgT_sz = NH * 512 * 2
    m_size = max(ymix_sz + max(sgT_sz, D * 4) + 2 * 4 * RW * 2 + 4096, FT * 512 * 2)
    pos[0] = m_off + m_size
    assert pos[0] <= 206 * 1024, "SBUF arena overflow %d" % pos[0]
    ymix = alloc_at(m_off, [128, KT, 512], BF16)
    sgT = alloc_at(m_off + ymix_sz, [128, NH, 512], BF16)
    xh = alloc_at(m_off + ymix_sz, [128, D], F32)
    actT = alloc_at(m_off, [128, FT, 512], BF16)
    kv_off = m_off + ymix_sz + max(sgT_sz, D * 4)
    k_rot = alloc_at(kv_off, [128, 4, NH, 128], BF16)
    v_sb = alloc_at(kv_off + 4 * RW * 2, [128, 4, NH, 128], BF16)
    h_bf2 = alloc_at(kv_off + 8 * RW * 2, [128, D], BF16)
    Rb32 = alloc_at(m_off + ymix_sz, [128, NH, 128], F32)
    hT2 = alloc_at(m_off, [128, KT, 512], BF16)
    stg = alloc_at(xr_off, [128, max(CW, 128)], F32)
    stg2 = alloc_at(xr_off + max(CW, 128) * 4, [128, 128], F32)
    dif = alloc_at(acc_off, [128, 128], F32)
    difi = alloc_at(acc_off + 512, [128, 128], I32)

    ps_t = [es.enter_context(nc.psum_tensor("ps%d" % i, [128, 512], F32)) for i in range(8)]

    def psap(t):
        return t.ap() if hasattr(t, "ap") and callable(getattr(t, "ap")) else t

    PB = [Buf(psap(ps_t[i]), "ps", i * 2048, [128, 512], 4) for i in range(8)]
    PB4 = [Buf(psap(ps_t[i]).rearrange("p (a b) -> p a b", a=4), "ps", i * 2048, [128, 4, 128], 4) for i in range(8)]
    PBb = [Buf(psap(ps_t[i]).bitcast(BF16).rearrange("p (a b) -> p a b", b=128), "ps", i * 2048, [128, 8, 128], 2) for i in range(8)]
    TR = [0, 1]
    MM = [2, 3, 4, 5]
    RT0, RT1 = 6, 7
    rot = {"tr": 0, "mm": 0, "slot": 0, "tmp": 0}

    def nxt(key, lst):
        i = rot[key]
        rot[key] = (i + 1) % len(lst)
        return lst[i]

    def fsz(v):
        n = 1
        for d in v.ap.shape[1:]:
            n *= d
        return n

    def bank(v):
        lo = v.lo // 2048 * 2048
        return View(v.ap, "ps", lo, lo + 2048)

    def mm(out, lhsT, rhs, start=True, stop=True):
        d = max(fsz(rhs), 64) / 2300.0 * (4.0 if rhs.ap.dtype == F32 else 1.0) + 0.005
        kk.op("pe", lambda e: e.matmul(out.ap, lhsT=lhsT.ap, rhs=rhs.ap, start=start, stop=stop), reads=[lhsT, rhs], writes=[bank(out)], dur=d)

    def trp(out, in_, ident):
        kk.op("pe", lambda e: e.transpose(out.ap, in_.ap, ident.ap), reads=[in_, ident], writes=[bank(out)], dur=0.1 * (4.0 if in_.ap.dtype == F32 else 1.0))

    def edur(eng, out):
        n = fsz(out)
        if eng == "act":
            return 0.2 + n / 1200.0
        if eng == "pool":
            return 0.5 + n / 200.0
        return 0.1 + n / 900.0

    def act(out, in_, func, scale=1.0, bias=0.0, accum=None, eng="act"):
        rds = [in_]
        kw = {}
        if isinstance(scale, View):
            rds.append(scale)
            kw["scale"] = scale.ap
        else:
            kw["scale"] = float(scale)
        if isinstance(bias, View):
            rds.append(bias)
            kw["bias"] = bias.ap
        elif bias != 0.0:
            kw["bias"] = float(bias)
        wr = [out]
        if accum is not None:
            wr.append(accum)
            kw["accum_out"] = accum.ap
        kk.op(eng, lambda e: e.activation(out=out.ap, in_=in_.ap, func=func, **kw), reads=rds, writes=wr, dur=edur(eng, out))

    def tt(out, in0, in1, op, eng="dve"):
        kk.op(eng, lambda e: e.tensor_tensor(out=out.ap, in0=in0.ap, in1=in1.ap, op=op), reads=[in0, in1], writes=[out], dur=edur(eng, out))

    def ts(out, in0, s1, s2, op0, op1=None, eng="dve"):
        rds = [in0]
        a1 = s1.ap if isinstance(s1, View) else float(s1)
        a2 = None if s2 is None else (s2.ap if isinstance(s2, View) else float(s2))
        for s in (s1, s2):
            if isinstance(s, View):
                rds.append(s)
        if op1 is None:
            kk.op(eng, lambda e: e.tensor_scalar(out=out.ap, in0=in0.ap, scalar1=a1, scalar2=None, op0=op0), reads=rds, writes=[out], dur=edur(eng, out))
        else:
            kk.op(eng, lambda e: e.tensor_scalar(out=out.ap, in0=in0.ap, scalar1=a1, scalar2=a2, op0=op0, op1=op1), reads=rds, writes=[out], dur=edur(eng, out))

    def stt(out, in0, sc, in1, op0, op1):
        rds = [in0, in1]
        a = sc.ap if isinstance(sc, View) else float(sc)
        if isinstance(sc, View):
            rds.append(sc)
        kk.op("dve", lambda e: e.scalar_tensor_tensor(out=out.ap, in0=in0.ap, scalar=a, in1=in1.ap, op0=op0, op1=op1), reads=rds, writes=[out], dur=edur("dve", out))

    def cp(out, in_, eng="dve"):
        if eng == "act":
            act(out, in_, AF.Copy)
        else:
            kk.op(eng, lambda e: e.tensor_copy(out=out.ap, in_=in_.ap), reads=[in_], writes=[out], dur=edur(eng, out))

    def mset(v, val, eng="dve"):
        kk.op(eng, lambda e: e.memset(v.ap, val), writes=[v], dur=edur(eng, v))

    def recip(out, in_):
        kk.op("dve", lambda e: e.reciprocal(out=out.ap, in_=in_.ap), reads=[in_], writes=[out], dur=edur("dve", out))

    def recip_big(out, in_, scratch):
        kk.op("act", lambda e: e.activation(out=scratch.ap, in_=in_.ap, func=AF.Ln), reads=[in_], writes=[scratch], dur=edur("act", out))
        kk.op("act", lambda e: e.activation(out=out.ap, in_=scratch.ap, func=AF.Exp, scale=-1.0), reads=[scratch], writes=[out], dur=edur("act", out))

    def dma(eng, out, in_, sem):
        v = out if out.ap is not None and not out.space.startswith("dr") else in_
        nb = fsz(v) * (2 if v.ap.dtype == BF16 else 4) * v.ap.shape[0]
        kk.op(eng, lambda e: e.dma_start(out=out.ap, in_=in_.ap), reads=[in_], writes=[out], dma_sem=sem, dur=2.0 + nb / 180e3)

    def bc(v, shape, axis):
        return v.m(lambda a: a.unsqueeze(axis).to_broadcast(list(shape)))

    dma("act", identf[:], dview(ident_d, "ident"), "d_c0")
    dma("act", small[:, c_link[0]:c_link[1]], dview(link_d, "link"), "d_c1")
    dma("act", wfb[:], dview(nf_d.partition_broadcast(128), "nf"), "d_c2")
    dma("act", S(c_lgf), dview(lgf_d.partition_broadcast(128), "lgf"), "d_c3")
    dma("act", S(c_lgb), dview(lgb_d.partition_broadcast(128), "lgb"), "d_c4")
    mset(stg[:], 0.0)
    mset(stg2[:], 0.0)
    mset(xh[:], 0.0)
    dma("act", View(stg.ap[0:31, 0:CW], "sb", stg.off, stg.off + CW * 4), dview(conv_w_d, "cvw"), "d_c5")
    dma("act", View(stg.ap[31:32, 0:CW], "sb", stg.off, stg.off + CW * 4), dview(conv_b_d, "cvb"), "d_c5")
    dma("act", View(stg.ap[32:33, 0:CW], "sb", stg.off, stg.off + CW * 4), dview(ln_w_d, "lnw"), "d_c5")
    dma("act", View(stg.ap[33:34, 0:CW], "sb", stg.off, stg.off + CW * 4), dview(ln_b_d, "lnb"), "d_c5")
    dma("act", View(stg2.ap[0:NH, :], "sb", stg2.off, stg2.off + 512), dview(rnw_d, "rnw"), "d_c6")
    dma("act", View(stg2.ap[32:32 + KT, :], "sb", stg2.off, stg2.off + 512), dview(n1_d, "n1"), "d_c6")
    dma("act", View(stg2.ap[64:64 + KT, :], "sb", stg2.off, stg2.off + 512), dview(n2_d, "n2"), "d_c6")
    cp(identb[:], identf[:])
    mset(ones_ln[:], 1.0 / CW)
    mset(ones_gn[:], 1.0 / 128)
    for c in range(CT):
        pb = PB[nxt("mm", MM)]
        trp(View(pb.ap[:, 0:34], "ps", pb.off, pb.off + 2048), View(stg.ap[0:34, c * 128:(c + 1) * 128], "sb", stg.off, stg.off + CW * 4),
            View(identf.ap[0:34, 0:34], "sb", identf.off, identf.off + 512))
        cp(cw[:, c, 0:34], pb[:, 0:34])
    for (r0, n, cc) in ((0, NH, c_rnw), (32, KT, c_n1), (64, KT, c_n2)):
        pb = PB[nxt("mm", MM)]
        trp(View(pb.ap[:, 0:n], "ps", pb.off, pb.off + 2048), View(stg2.ap[r0:r0 + n, :], "sb", stg2.off, stg2.off + 512),
            View(identf.ap[r0:r0 + n, r0:r0 + n], "sb", identf.off, identf.off + 512))
        cp(S(cc), pb[:, 0:n])
    sc = 128.0 ** -0.5
    kk.op("pool", lambda e: e.iota(difi.ap, pattern=[[1, 128]], base=0, channel_multiplier=-1), writes=[difi[:]])
    cp(dif[:], difi[:])
    for h in range(NH):
        ts(tmp[0][:, 0:128], dif[:], 0.0, None, ALU.max)
        act(tmp[0][:, 0:128], tmp[0][:, 0:128], AF.Exp, scale=S(c_lgf, h))
        ts(tmp[1][:, 0:128], dif[:], 0.0, sc, ALU.is_ge, ALU.mult)
        tt(tmp[0][:, 0:128], tmp[0][:, 0:128], tmp[1][:, 0:128], ALU.mult)
        ts(tmp[2][:, 0:128], dif[:], -1.0, 0.0, ALU.mult, ALU.max)
        act(tmp[2][:, 0:128], tmp[2][:, 0:128], AF.Exp, scale=S(c_lgb, h))
        ts(tmp[1][:, 0:128], dif[:], 0.0, sc, ALU.is_lt, ALU.mult)
        tt(tmp[2][:, 0:128], tmp[2][:, 0:128], tmp[1][:, 0:128], ALU.mult)
        tt(DcT[:, h, :], tmp[0][:, 0:128], tmp[2][:, 0:128], ALU.add)
    kk.op("pool", lambda e: e.iota(difi.ap, pattern=[[1, 128]], base=1, channel_multiplier=0), writes=[difi[:]])
    cp(dif[:], difi[:])
    for h in range(NH):
        act(XIf[:, h, :], dif[:], AF.Exp, scale=S(c_lgf, h))
    kk.op("pool", lambda e: e.iota(difi.ap, pattern=[[-1, 128]], base=128, channel_multiplier=0), writes=[difi[:]])
    cp(dif[:], difi[:])
    for h in range(NH):
        act(XIb[:, h, :], dif[:], AF.Exp, scale=S(c_lgb, h))
    kk.op("pool", lambda e: e.iota(difi.ap[:, 0:1], pattern=[[0, 1]], base=127, channel_multiplier=-1), writes=[difi[:]])
    cp(S(c_pj), difi[:, 0:1])
    kk.op("pool", lambda e: e.iota(difi.ap[:, 0:1], pattern=[[0, 1]], base=0, channel_multiplier=1), writes=[difi[:]])
    cp(S(c_pj2), difi[:, 0:1])
    act(S(c_zf), S(c_lgf), AF.Exp, scale=S(c_pj))
    ts(S(c_zf), S(c_zf), sc, None, ALU.mult)
    act(S(c_zb), S(c_lgb), AF.Exp, scale=S(c_pj2))
    ts(S(c_zb), S(c_zb), sc, None, ALU.mult)
    act(S(c_cdf), S(c_lgf), AF.Exp, scale=128.0)
    act(S(c_cdb), S(c_lgb), AF.Exp, scale=128.0)

    STOP = cfg.get("stop", 99)
    cfg_lat[0] = cfg.get("lat", 1.2)

    def finish():
        kk.emit(nc, es, reorder=cfg.get("reorder", True))
        es.close()
        return nc

    if STOP <= 1:
        return finish()
    ncv = [0]

    def conv_unit(dst, src2d, k0, nk, c0, ncol, name):
        src = src2d[k0 * 128:(k0 + nk) * 128, c0:c0 + ncol].rearrange("(k p) c -> p k c", p=128)
        d = dst.rearrange("p (k c) -> p k c", k=nk)
        s = "d_w%d" % (ncv[0] % 4)
        ncv[0] += 1
        kk.op("pool", lambda e: e.dma_start(out=d, in_=src), reads=[dview(None, "src_" + name)], writes=[dview(None, name), dview(None, "semx_" + s)], dma_sem=s,
              dur=2.0 + nk * 128 * ncol * 6 / 150e3)

    def qkv_unit(cc, half):
        return cc * 2 + half

    order = []
    for cc in range(RW // 512, 3 * RW // 512):
        for half in range(2):
            order.append(("qkv", cc, half))
    for u in range(n_cv // 2):
        order.append(("cv", u, 0))
        order.append(("cv", u, 1))
    for cc in range(RW // 512):
        for half in range(2):
            order.append(("qkv", cc, half))
    for u in range(n_g):
        order.append(("g", u, 0))
    for cc in range(D // 512):
        for half in range(2):
            order.append(("wo", cc, half))
    for u in range(n_f1 // 2):
        order.append(("f1", u, 0))
        order.append(("f1", u, 1))
    for cc in range(D // 512):
        for q in range(KQ):
            order.append(("f2", cc, q))
    for (kind, a, b) in order:
        if kind == "qkv":
            u = qkv_unit(a, b)
            conv_unit(s_qkv[u], w_in_d, b * KH, KH, 2 * CW + a * 512, 512, "qkv%d" % u)
        elif kind == "cv":
            conv_unit(s_cv[2 * a + b], w_in_d, 0, KT, b * CW + a * 256, 256, "cv%d" % (2 * a + b))
        elif kind == "g":
            conv_unit(s_g[a], w_in_d, 0, KT, 2 * CW + 3 * RW + a * 256, 256, "g%d" % a)
        elif kind == "wo":
            u = a * 2 + b
            conv_unit(s_wo[u], w_out_d, b * KH, KH, a * 512, 512, "wo%d" % u)
        elif kind == "f1":
            conv_unit(s_f1[2 * a + b], w_f1_d, 0, KT, b * DFF + a * 256, 256, "f1%d" % (2 * a + b))
        elif kind == "f2":
            u = a * KQ + b
            conv_unit(s_f2[u][:, 0:F2K[b] * 512], w_f2_d, b * 8, F2K[b], a * 512, 512, "f2%d" % u)

    if STOP <= 2:
        return finish()

    def wload(src_ap, name, nelem):
        i = nxt("slot", list(range(NSLOT)))
        sl = ring[i]
        dma("sp", sl[:, 0:nelem], dview(src_ap, name), "d_r%d" % i)
        return sl

    def seg_of_chunk(n):
        return n // CPS

    def load_x(b):
        for t in range(4):
            r0 = b * 512 + t * 128
            dma("act", xt[:, t, :], dview(x_d[r0:r0 + 128, :], "x"), "d_x%d" % t)

    def norm_T(b, cn, halo, stream=False, main=True, dstT=None):
        dstT = hT if dstT is None else dstT
        jobs = []
        if main:
            jobs = [((xs[:] if stream else xt[:, t, :]), 128, t) for t in range(4)]
        if halo:
            jobs.append((View(xh.ap[0:32, :], "sb", xh.off, xh.off + D * 4), 32, None))
        for ji, (xv, rows, t) in enumerate(jobs):
            if stream and t is not None:
                r0 = b * 512 + t * 128
                dma("act", xs[:], dview(x_d[r0:r0 + 128, :], "x"), "d_xs")
            hbuf = h_bf if stream else (h_bf, h_bf2)[ji % 2]
            hb = View(hbuf.ap[0:rows, :], "sb", hbuf.off, hbuf.off + D * 2)
            ti = 4 if t is None else t
            ssv = View(small.ap[0:rows, c_ss[0] + ti:c_ss[0] + ti + 1], "sb", small.off, small.off + 512)
            rsv = View(small.ap[0:rows, c_rs[0] + ti:c_rs[0] + ti + 1], "sb", small.off, small.off + 512)
            mset(ssv, 0.0)
            act(hb, xv, AF.Square, accum=ssv)
            act(rsv, ssv, AF.Sqrt, scale=1.0 / D, bias=EPS)
            recip(rsv, rsv)
            if stream:
                act(hb, xv, AF.Copy, scale=rsv)
            else:
                ts(hb, xv, rsv, None, ALU.mult)
            for kg in range(KT // 4):
                pb = PBb[nxt("tr", TR)]
                for j in range(4):
                    k = kg * 4 + j
                    trp(View(pb.ap[:, j, 0:rows], "ps", pb.off, pb.off + 1024),
                        View(hbuf.ap[0:rows, k * 128:(k + 1) * 128], "sb", hbuf.off + k * 256, hbuf.off + (k + 1) * 256),
                        View(identb.ap[0:rows, 0:rows], "sb", identb.off, identb.off + 256))
                for j in range(4):
                    k = kg * 4 + j
                    dst = dstT[:, k, t * 128:(t + 1) * 128] if t is not None else hTh[:, k, :]
                    act(dst, View(pb.ap[:, j, 0:rows], "ps", pb.off, pb.off + 1024), AF.Copy, scale=S(cn, k))

    def tm_stage(units, lhs_fn, evac_fn):
        nsplit = len(units)
        k0 = 0
        for si, (src, name, nk) in enumerate(units):
            sl = wload(src, name, nk * 512)
            slv = Buf(sl.ap[:, 0:nk * 512].rearrange("p (k c) -> p k c", k=nk), "sb", sl.off, [128, nk, 512], 2)
            for t in range(4):
                for k in range(nk):
                    mm(PB[MM[t]][:, :], lhs_fn(k0 + k, t), slv[:, k, :], start=(si == 0 and k == 0), stop=(si == nsplit - 1 and k == nk - 1))
            k0 += nk
        for t in range(4):
            evac_fn(t, MM[t])

    def rotary(dst, t, h0, pbi):
        P4 = PB4[pbi]
        x1 = P4[:, :, 0:64]
        x2 = P4[:, :, 64:128]
        cb = bc(cos4[:, t, :], [128, 4, 64], 1)
        sb_ = bc(sin4[:, t, :], [128, 4, 64], 1)
        tt(rt[0][:], x1, cb, ALU.mult)
        tt(rt[1][:], x2, sb_, ALU.mult)
        tt(dst[:, t, h0:h0 + 4, 0:64], rt[0][:], rt[1][:], ALU.subtract)
        tt(rt[0][:], x1, sb_, ALU.mult)
        tt(rt[1][:], x2, cb, ALU.mult)
        tt(dst[:, t, h0:h0 + 4, 64:128], rt[0][:], rt[1][:], ALU.add)

    def load_cs(b):
        dma("act", cos4[:], dview(cos_d[b * 4:b * 4 + 4].rearrange("c p f -> p c f"), "cos"), "d_cos")
        dma("act", sin4[:], dview(sin_d[b * 4:b * 4 + 4].rearrange("c p f -> p c f"), "sin"), "d_sin")

    def qkv_proj(b, which, srcT=None):
        srcT = hT if srcT is None else srcT
        nq = RW // 512
        for cc in which:
            units = [(s_qkv[cc * 2 + half], "qkv%d" % (cc * 2 + half), KH) for half in range(2)]
            kind = cc // nq
            h0 = (cc % nq) * 4

            def evac(t, pbi, kind=kind, h0=h0):
                if kind == 0:
                    rotary(q_rot, t, h0, pbi)
                elif kind == 1:
                    rotary(k_rot, t, h0, pbi)
                else:
                    act(v_sb[:, t, h0:h0 + 4, :], PB4[pbi][:], AF.Copy)

            tm_stage(units, lambda k, t: srcT[:, k, t * 128:(t + 1) * 128], evac)

    for b in range(NB - 1, -1, -1):
        load_x(b)
        load_cs(b)
        hTp = (hT, hT2)[b % 2]
        norm_T(b, c_n1, False, dstT=hTp)
        qkv_proj(b, list(range(RW // 512, 3 * RW // 512)), srcT=hTp)
        dma("act", dview(kvs_d[b, 0], "kvk%d" % b), k_rot[:], "d_kvk")
        dma("act", dview(kvs_d[b, 1], "kvv%d" % b), v_sb[:], "d_kvv")
        for t in range(3, -1, -1):
            n = b * 4 + t
            if n % CPS == CPS - 1:
                if seg_of_chunk(n) == 0 and n != NCH - 1:
                    ts(Rb32[:], Rb32[:], S(c_link), None, ALU.mult)
                else:
                    mset(Rb32[:], 0.0)
            st = Rb_bf[n % 2]
            cp(st[:], Rb32[:], eng="act")
            dma("act", dview(rbs_d[n], "rb%d" % n), st[:], "d_rb%d" % (n % 2))
            for hg in range(NHG):
                hs = slice(hg * 4, hg * 4 + 4)
                kz = kzs[(n * NHG + hg) % 2]
                for j in range(4):
                    act(kz[:, j, :], k_rot[:, t, hg * 4 + j, :], AF.Copy, scale=S(c_zb, hg * 4 + j))
                pbi = nxt("mm", MM)
                for j in range(4):
                    mm(PB4[pbi][:, j, :], kz[:, j, :], v_sb[:, t, hg * 4 + j, :])
                for j in range(4):
                    h = hg * 4 + j
                    stt(Rb32[:, h, :], Rb32[:, h, :], S(c_cdb, h), PB4[pbi][:, j, :], ALU.mult, ALU.add)

    if STOP <= 3:
        return finish()
    for b in range(NB):
        if STOP == 4 and b == 1:
            return finish()
        seg = b // SEGB
        first_in_seg = (b % SEGB == 0)
        last_in_seg = (b % SEGB == SEGB - 1)
        if first_in_seg:
            lmode = "link" if seg == 1 else "zero"
        else:
            lmode = "real"
        if last_in_seg:
            rmode = "link" if seg == 0 and cfg["NSEG"] > 1 else "zero"
        else:
            rmode = "real"
        halo = (lmode != "zero") or (rmode != "zero")
        if b == 0:
            load_cs(0)
            norm_T(0, c_n1, False, stream=True)
        if halo:
            if lmode != "zero":
                dma("act", View(xh.ap[0:16, :], "sb", xh.off, xh.off + D * 4), dview(x_d[b * 512 - 16:b * 512, :], "x"), "d_xh")
            if rmode != "zero":
                dma("act", View(xh.ap[16:32, :], "sb", xh.off, xh.off + D * 4), dview(x_d[b * 512 + 512:b * 512 + 528, :], "x"), "d_xh")
        if halo:
            norm_T(b, c_n1, True, main=False)
        if STOP == 41:
            return finish()
        for u in range(n_cv // 2):
            sa = wload(s_cv[2 * u], "cv%d" % (2 * u), KT * 256)
            sg_ = wload(s_cv[2 * u + 1], "cv%d" % (2 * u + 1), KT * 256)
            sav = Buf(sa.ap[:, 0:KT * 256].rearrange("p (k c) -> p k c", k=KT), "sb", sa.off, [128, KT, 256], 2)
            sgv = Buf(sg_.ap[:, 0:KT * 256].rearrange("p (k c) -> p k c", k=KT), "sb", sg_.off, [128, KT, 256], 2)
            for m_ in range(2):
                c = 2 * u + m_
                pa = nxt("mm", MM)
                pg = nxt("mm", MM)
                for k in range(KT):
                    mm(PB[pa][:, :], sav[:, k, m_ * 128:(m_ + 1) * 128], hT[:, k, :], start=(k == 0), stop=(k == KT - 1))
                for k in range(KT):
                    mm(PB[pg][:, :], sgv[:, k, m_ * 128:(m_ + 1) * 128], hT[:, k, :], start=(k == 0), stop=(k == KT - 1))
                if halo:
                    for k in range(KT):
                        mm(PB[RT0][:, 0:32], sav[:, k, m_ * 128:(m_ + 1) * 128], hTh[:, k, :], start=(k == 0), stop=(k == KT - 1))
                    for k in range(KT):
                        mm(PB[RT0][:, 128:160], sgv[:, k, m_ * 128:(m_ + 1) * 128], hTh[:, k, :], start=(k == 0), stop=(k == KT - 1))
                tp = tmp[nxt("tmp", [0, 1])]
                act(tp[:], PB[pg][:, :], AF.Sigmoid)
                tt(glu[:, c, 15:527], PB[pa][:, :], tp[:], ALU.mult)
                if halo:
                    act(tmp[2][:, 0:32], PB[RT0][:, 128:160], AF.Sigmoid)
                for (mode, dst, sl_) in ((lmode, glu[:, c, 0:15], slice(1, 16)), (rmode, glu[:, c, 527:542], slice(16, 31))):
                    if mode == "zero":
                        mset(dst, 0.0)
                    elif mode == "real":
                        tt(dst, PB[RT0][:, sl_], tmp[2][:, sl_], ALU.mult)
                    else:
                        stt(dst, PB[RT0][:, sl_], S(c_link), tmp[2][:, sl_], ALU.mult, ALU.mult)
        def conv_pairs(p0, p1):
            for c0 in range(2 * p0, min(2 * p1, CT), 2):
                for j in range(31):
                    for c in range(c0, min(c0 + 2, CT)):
                        if j == 0:
                            ts(acc[:, c, :], glu[:, c, 0:512], cw[:, c, 0:1], cw[:, c, 31:32], ALU.mult, ALU.add)
                        else:
                            stt(acc[:, c, :], glu[:, c, j:j + 512], cw[:, c, j:j + 1], acc[:, c, :], ALU.mult, ALU.add)

        NP_EARLY = cfg.get("np_early", 2)
        conv_pairs(0, NP_EARLY)
        if STOP == 44:
            return finish()
        dma("act", k_rot[:], dview(kvs_d[b, 0], "kvk%d" % b), "d_kvk")
        dma("act", v_sb[:], dview(kvs_d[b, 1], "kvv%d" % b), "d_kvv")
        qkv_proj(b, list(range(RW // 512)))
        if STOP == 45:
            return finish()
        for u in range(n_g):
            sl = wload(s_g[u], "g%d" % u, KT * 256)
            slv = Buf(sl.ap[:, 0:KT * 256].rearrange("p (k c) -> p k c", k=KT), "sb", sl.off, [128, KT, 256], 2)
            for m_ in range(2):
                h = 2 * u + m_
                pg = nxt("mm", MM)
                for k in range(KT):
                    mm(PB[pg][:, :], slv[:, k, m_ * 128:(m_ + 1) * 128], hT[:, k, :], start=(k == 0), stop=(k == KT - 1))
                tp = tmp[nxt("tmp", [0, 1])]
                act(tp[:], PB[pg][:, :], AF.Silu)
                act(sgT[:, h, :], tp[:], AF.Copy, scale=S(c_rnw, h))
        if STOP == 46:
            return finish()
        for t in range(4):
            n = b * 4 + t
            rbuf = Rb_bf[n % 2]
            dma("act", rbuf[:], dview(rbs_d[n], "rb%d" % n), "d_rb%d" % (n % 2))
            if n % CPS == 0:
                if seg_of_chunk(n) == 1:
                    ts(Rf32[:], Rf32[:], S(c_link), None, ALU.mult)
                else:
                    mset(Rf32[:], 0.0)
                cp(Rf_bf[:], Rf32[:], eng="act")
            for hg in range(NHG):
                hs = slice(hg * 4, hg * 4 + 4)
                par = (n * NHG + hg) % 2
                PT = PTs[par]
                kz = kzs[par]
                tA, tB = tmp[2 * par], tmp[2 * par + 1]
                bS = (RT0, RT1)[par]
                bO = MM[par]
                bX = MM[2 + par]
                pq = PBb[nxt("tr", TR)]
                for j in range(4):
                    trp(pq[:, j, :], q_rot[:, t, hg * 4 + j, :], identb[:])
                pk = PBb[nxt("tr", TR)]
                for j in range(4):
                    trp(pk[:, j, :], k_rot[:, t, hg * 4 + j, :], identb[:])
                if STOP == 4601:
                    return finish()
                act(qT[:, hs, :], pq[:, 0:4, :], AF.Copy)
                tt(qfT[:, hs, :], qT[:, hs, :], XIf[:, hs, :], ALU.mult)
                tt(qbT[:, hs, :], qT[:, hs, :], XIb[:, hs, :], ALU.mult)
                act(kT[:, hs, :], pk[:, 0:4, :], AF.Copy)
                for j in range(4):
                    mm(PB4[bS][:, j, :], kT[:, hg * 4 + j, :], qT[:, hg * 4 + j, :])
                tt(PT[:], PB4[bS][:], DcT[:, hs, :], ALU.mult)
                for j in range(4):
                    h = hg * 4 + j
                    mm(PB4[bO][:, j, :], v_sb[:, t, h, :], PT[:, j, :], start=True, stop=False)
                    mm(PB4[bO][:, j, :], Rf_bf[:, h, :], qfT[:, h, :], start=False, stop=False)
                    mm(PB4[bO][:, j, :], rbuf[:, h, :], qbT[:, h, :], start=False, stop=True)
                act(tA[:], PB[bO][:, :], AF.Square)
                mm(PB[bX][:, :], ones_gn[:], tA[:])
                act(tB[:], PB[bX][:, :], AF.Sqrt, bias=EPS)
                recip_big(tB[:], tB[:], tA[:])
                tt(tA[:], PB[bO][:, :], tB[:], ALU.mult)
                tAv = View(tA.ap.rearrange("p (a b) -> p a b", a=4), "sb", tA.off, tA.off + 2048)
                tt(ymix[:, CT + hg * 4:CT + hg * 4 + 4, t * 128:(t + 1) * 128], tAv, sgT[:, hs, t * 128:(t + 1) * 128], ALU.mult)
                for j in range(4):
                    act(kz[:, j, :], k_rot[:, t, hg * 4 + j, :], AF.Copy, scale=S(c_zf, hg * 4 + j))
                for j in range(4):
                    mm(PB4[bX][:, j, :], kz[:, j, :], v_sb[:, t, hg * 4 + j, :])
                for j in range(4):
                    h = hg * 4 + j
                    stt(Rf32[:, h, :], Rf32[:, h, :], S(c_cdf, h), PB4[bX][:, j, :], ALU.mult, ALU.add)
                cp(Rf_bf[:, hs, :], Rf32[:, hs, :], eng="act")
        if STOP == 42:
            return finish()
        conv_pairs(NP_EARLY, (CT + 1) // 2)
        if STOP == 43:
            return finish()
        pm = nxt("mm", MM)
        pe2 = nxt("mm", MM)
        for c in range(CT):
            tp = tmp[nxt("tmp", [0, 1])]
            act(tp[:], acc[:, c, :], AF.Square)
            mm(PB[pm][:, :], ones_ln[:], acc[:, c, :], start=(c == 0), stop=(c == CT - 1))
            mm(PB[pe2][:, :], ones_ln[:], tp[:], start=(c == 0), stop=(c == CT - 1))
        act(tmp[2][:], PB[pm][:, :], AF.Copy)
        tt(tmp[3][:], tmp[2][:], tmp[2][:], ALU.mult)
        tt(tmp[3][:], PB[pe2][:, :], tmp[3][:], ALU.subtract)
        act(tmp[0][:], tmp[3][:], AF.Sqrt, bias=EPS)
        recip_big(tmp[3][:], tmp[0][:], tmp[1][:])
        for c in range(CT):
            tp = tmp[nxt("tmp", [0, 1])]
            tt(tp[:], acc[:, c, :], tmp[2][:], ALU.subtract)
            tt(tp[:], tp[:], tmp[3][:], ALU.mult)
            act(ymix[:, c, :], tp[:], AF.Silu, scale=cw[:, c, 32:33], bias=cw[:, c, 33:34])
        if STOP == 47:
            return finish()
        load_x(b)
        for cc in range(D // 512):
            units = [(s_wo[cc * 2 + half], "wo%d" % (cc * 2 + half), KH) for half in range(2)]

            def evac9(t, pbi, cc=cc):
                tt(xt[:, t, cc * 512:(cc + 1) * 512], PB[pbi][:, :], xt[:, t, cc * 512:(cc + 1) * 512], ALU.add)

            tm_stage(units, lambda k, t: ymix[:, k, t * 128:(t + 1) * 128], evac9)
        if STOP == 48:
            return finish()
        norm_T(b, c_n2, False)
        if STOP == 49:
            return finish()
        for u in range(n_f1 // 2):
            sa = wload(s_f1[2 * u], "f1%d" % (2 * u), KT * 256)
            sb2 = wload(s_f1[2 * u + 1], "f1%d" % (2 * u + 1), KT * 256)
            sav = Buf(sa.ap[:, 0:KT * 256].rearrange("p (k c) -> p k c", k=KT), "sb", sa.off, [128, KT, 256], 2)
            sbv = Buf(sb2.ap[:, 0:KT * 256].rearrange("p (k c) -> p k c", k=KT), "sb", sb2.off, [128, KT, 256], 2)
            for m_ in range(2):
                f = 2 * u + m_
                pg = nxt("mm", MM)
                pu = nxt("mm", MM)
                for k in range(KT):
                    mm(PB[pg][:, :], sav[:, k, m_ * 128:(m_ + 1) * 128], hT[:, k, :], start=(k == 0), stop=(k == KT - 1))
                for k in range(KT):
                    mm(PB[pu][:, :], sbv[:, k, m_ * 128:(m_ + 1) * 128], hT[:, k, :], start=(k == 0), stop=(k == KT - 1))
                tp = tmp[nxt("tmp", [0, 1])]
                act(tp[:], PB[pg][:, :], AF.Silu)
                tt(actT[:, f, :], PB[pu][:, :], tp[:], ALU.mult)
        if STOP == 50:
            return finish()
        if b + 1 < NB:
            load_cs(b + 1)
            norm_T(b + 1, c_n1, False, stream=True)
        for cc in range(D // 512):
            units = [(s_f2[cc * KQ + q][:, 0:F2K[q] * 512], "f2%d" % (cc * KQ + q), F2K[q]) for q in range(KQ)]

            def evac12(t, pbi, cc=cc):
                tt(xt[:, t, cc * 512:(cc + 1) * 512], PB[pbi][:, :], xt[:, t, cc * 512:(cc + 1) * 512], ALU.add)

            tm_stage(units, lambda k, t: actT[:, k, t * 128:(t + 1) * 128], evac12)
        if STOP == 51:
            return finish()
        for t in range(4):
            mset(S(c_ss, t), 0.0)
            act(h_bf[:], xt[:, t, :], AF.Square, accum=S(c_ss, t))
            ts(S(c_rs, t), S(c_ss, t), 1.0 / D, EPS, ALU.mult, ALU.add)
            act(S(c_rs, t), S(c_rs, t), AF.Sqrt)
            recip(S(c_rs, t), S(c_rs, t))
            r0 = b * 512 + t * 128
            for hf in range(2):
                cs_ = slice(hf * (D // 2), (hf + 1) * (D // 2))
                stt(xs[:, cs_], xt[:, t, cs_], S(c_rs, t), wfb[:, cs_], ALU.mult, ALU.mult)
                dma("act", dview(y_d[r0:r0 + 128, cs_], "y%d_%d" % (r0, hf)), xs[:, cs_], "d_ys%d" % hf)

    return finish()


def rope_tables(positions):
    inv_freq = (10000.0 ** (-np.arange(0, 128, 2, dtype=np.float32) / np.float32(128))).astype(np.float32)
    ang = positions.astype(np.float32)[:, None] * inv_freq[None, :]
    return np.cos(ang).astype(np.float32), np.sin(ang).astype(np.float32)


def core_layout(cfg, n_prompt, n_sample, seq_p):
    out = []
    for c in range(NCORES):
        if c < n_sample:
            segs = [("s", c, 0), ("s", c, seq_p), ("p", c, 0)]
            link = 1.0
        else:
            j = c - n_sample
            base = n_sample + 3 * j
            segs = [("p", base, 0), ("p", base + 1, 0), ("p", base + 2, 0)]
            link = 0.0
        out.append((segs, link))
    return out


_NC_CACHE = {}


def kernel(x_prompt, x_sample, norm1_w, w_in, conv_w, conv_b, conv_ln_w, conv_ln_b,
           ret_log_decay_fwd, ret_log_decay_bwd, ret_norm_w, w_out,
           norm2_w, w_ffn_in, w_ffn_out, final_norm_w):
    x_prompt = np.asarray(x_prompt, dtype=np.float32)
    x_sample = np.asarray(x_sample, dtype=np.float32)
    Bp, Sp, D = x_prompt.shape
    Bs, Ss, _ = x_sample.shape
    DFF = np.asarray(w_ffn_out).shape[1]
    assert Ss == 2 * Sp and Bs * 2 + Bp == 3 * NCORES
    cfg = make_cfg(D=D, DFF=DFF, SEGB=Sp // 512, NSEG=3)
    key = (D, DFF, Sp)
    if key not in _NC_CACHE:
        _NC_CACHE[key] = build_nc(cfg)
    nc = _NC_CACHE[key]
    lay = core_layout(cfg, Bp, Bs, Sp)
    f = lambda a: np.ascontiguousarray(np.asarray(a, dtype=np.float32))
    shared = {
        "w_in": f(w_in)[0], "w_out": f(w_out)[0], "w_ffn_in": f(w_ffn_in)[0], "w_ffn_out": f(w_ffn_out)[0],
        "conv_w": f(conv_w)[0], "conv_b": f(conv_b).reshape(1, -1), "conv_ln_w": f(conv_ln_w).reshape(1, -1),
        "conv_ln_b": f(conv_ln_b).reshape(1, -1), "lgf": f(ret_log_decay_fwd).reshape(-1), "lgb": f(ret_log_decay_bwd).reshape(-1),
        "ret_norm_w": f(ret_norm_w).reshape(-1, 128), "norm1_w": f(norm1_w).reshape(-1, 128), "norm2_w": f(norm2_w).reshape(-1, 128),
        "final_norm_w": f(final_norm_w).reshape(-1), "ident": np.eye(128, dtype=np.float32),
    }
    in_maps = []
    for c in range(NCORES):
        segs, link = lay[c]
        xs, ps = [], []
        for (which, bi, r0) in segs:
            src = x_sample if which == "s" else x_prompt
            xs.append(src[bi, r0:r0 + Sp])
            ps.append(np.arange(r0, r0 + Sp))
        cos, sin = rope_tables(np.concatenate(ps))
        m = dict(shared)
        m["x"] = np.ascontiguousarray(np.concatenate(xs, axis=0))
        m["link"] = np.full((128, 1), link, dtype=np.float32)
        m["cos_t"] = np.ascontiguousarray(cos.reshape(-1, 128, 64))
        m["sin_t"] = np.ascontiguousarray(sin.reshape(-1, 128, 64))
        in_maps.append(m)
    res = run_bass_kernel_spmd(nc, in_maps, core_ids=list(range(NCORES)))
    y_prompt = np.empty_like(x_prompt)
    y_sample = np.empty_like(x_sample)
    for c in range(NCORES):
        segs, _ = lay[c]
        yc = res.results[c]["y"]
        for i, (which, bi, r0) in enumerate(segs):
            dst = y_sample if which == "s" else y_prompt
            dst[bi, r0:r0 + Sp] = yc[i * Sp:(i + 1) * Sp]
    return (y_prompt, y_sample)
```
